# Optimizing a Trainium2 kernel written in Bass

```python
import math
import jax, jax.numpy as jnp
from jax import lax
import numpy as np

D_MODEL = 2048
BATCH = 2
SEQ = 16384
DEPTH = 2

CHUNK = 64
LEFT_CHUNKS = 8
BAND = LEFT_CHUNKS + 1

MIX_WIDTH = D_MODEL
ATTN_WIDTH = MIX_WIDTH // 2
POOL_WIDTH = MIX_WIDTH - ATTN_WIDTH
HEAD_DIM = 128
N_HEADS = ATTN_WIDTH // HEAD_DIM
REL_CLIP = 128
POOL_WINDOWS = (2, 4, 8, 16)
N_POOL_GROUPS = len(POOL_WINDOWS)
POOL_GROUP = POOL_WIDTH // N_POOL_GROUPS
IN_WIDTH = 3 * ATTN_WIDTH + POOL_WIDTH

D_FF = 5632
CONV_WIDTH = 3
NORM_EPS = 1e-6

kernel_name = "hybrid_chunked_attn_multiscale_pool_convffn"


def rms_norm(x, g):
    xf = x.astype(jnp.float32)
    y = xf * lax.rsqrt(jnp.mean(xf * xf, axis=-1, keepdims=True) + NORM_EPS)
    return (y * g.astype(jnp.float32)).astype(x.dtype)


def _band_bias_index():
    q_pos = np.arange(CHUNK) + LEFT_CHUNKS * CHUNK
    k_pos = np.arange(BAND * CHUNK)
    rel = np.clip(q_pos[:, None] - k_pos[None, :], -REL_CLIP, REL_CLIP)
    return (rel + REL_CLIP).astype(np.int32)


def chunked_attention(q, k, v, rel_bias):
    B, S, H, Dh = q.shape
    N = S // CHUNK
    qc = q.reshape(B, N, CHUNK, H, Dh)
    pad = ((0, 0), (LEFT_CHUNKS * CHUNK, 0), (0, 0), (0, 0))
    kc = jnp.pad(k, pad).reshape(B, N + LEFT_CHUNKS, CHUNK, H, Dh)
    vc = jnp.pad(v, pad).reshape(B, N + LEFT_CHUNKS, CHUNK, H, Dh)
    scale = 1.0 / math.sqrt(Dh)
    scores = jnp.concatenate(
        [jnp.einsum('bnqhd,bnkhd->bnhqk', qc, kc[:, j:j + N],
                    preferred_element_type=jnp.float32) for j in range(BAND)],
        axis=-1) * scale
    bias = rel_bias.astype(jnp.float32)[:, _band_bias_index()]
    scores = scores + bias[None, None]
    key_chunk = np.repeat(np.arange(BAND), CHUNK)
    valid = (jnp.arange(N)[:, None] + key_chunk[None, :]) >= LEFT_CHUNKS
    scores = jnp.where(valid[None, :, None, None, :], scores, jnp.float32(-1e30))
    probs = jax.nn.softmax(scores, axis=-1).astype(v.dtype)
    out = 0.0
    for j in range(BAND):
        out = out + jnp.einsum('bnhqk,bnkhd->bnqhd',
                               probs[..., j * CHUNK:(j + 1) * CHUNK], vc[:, j:j + N])
    return out.reshape(B, S, H, Dh)


def multiscale_pool(u, pool_w, pool_scale):
    B, S, _ = u.shape
    uf = u.astype(jnp.float32)
    cs = jnp.cumsum(uf, axis=1)
    pos_count = jnp.arange(1, S + 1, dtype=jnp.float32)
    groups = []
    for g, w in enumerate(POOL_WINDOWS):
        sl = slice(g * POOL_GROUP, (g + 1) * POOL_GROUP)
        csg = cs[..., sl]
        shifted = jnp.pad(csg, ((0, 0), (w, 0), (0, 0)))[:, :S]
        count = jnp.minimum(pos_count, jnp.float32(w))[None, :, None]
        groups.append((csg - shifted) / count - uf[..., sl])
    pooled = jnp.stack(groups, axis=2).astype(u.dtype)
    mixed = jnp.einsum('bsgc,gcd->bsgd', pooled, pool_w).reshape(B, S, POOL_WIDTH)
    return mixed * pool_scale


def causal_dwconv(u, conv_w, conv_b):
    S = u.shape[1]
    up = jnp.pad(u, ((0, 0), (CONV_WIDTH - 1, 0), (0, 0)))
    out = conv_b
    for t in range(CONV_WIDTH):
        out = out + up[:, t:t + S] * conv_w[t]
    return out


def setup_inputs(seed: int = 0) -> dict:
    key = jax.random.key(seed)
    ks = jax.random.split(key, 16)
    f32 = jnp.float32
    nrm = lambda k, shape, s: jax.random.normal(k, shape, f32) * s
    gain = lambda k, n: 1.0 + 0.02 * jax.random.normal(k, (DEPTH, n), f32)
    return {
        "x": nrm(ks[0], (BATCH, SEQ, D_MODEL), 1.0),
        "pre_mix_g": gain(ks[1], D_MODEL),
        "w_in": nrm(ks[2], (DEPTH, D_MODEL, IN_WIDTH), D_MODEL ** -0.5),
        "rel_bias": nrm(ks[3], (DEPTH, N_HEADS, 2 * REL_CLIP + 1), 0.1),
        "pool_w": nrm(ks[4], (DEPTH, N_POOL_GROUPS, POOL_GROUP, POOL_GROUP), POOL_GROUP ** -0.5),
        "pool_scale": gain(ks[5], POOL_WIDTH),
        "w_o": nrm(ks[6], (DEPTH, MIX_WIDTH, D_MODEL), MIX_WIDTH ** -0.5),
        "post_mix_g": gain(ks[7], D_MODEL),
        "pre_ffn_g": gain(ks[8], D_MODEL),
        "w_up": nrm(ks[9], (DEPTH, D_MODEL, 2 * D_FF), D_MODEL ** -0.5),
        "conv_w": nrm(ks[10], (DEPTH, CONV_WIDTH, 2 * D_FF), CONV_WIDTH ** -0.5),
        "conv_b": nrm(ks[11], (DEPTH, 2 * D_FF), 0.01),
        "w_down": nrm(ks[12], (DEPTH, D_FF, D_MODEL), D_FF ** -0.5),
        "post_ffn_g": gain(ks[13], D_MODEL),
    }


def reference(x, pre_mix_g, w_in, rel_bias, pool_w, pool_scale, w_o, post_mix_g,
              pre_ffn_g, w_up, conv_w, conv_b, w_down, post_ffn_g):
    B, S, _ = x.shape
    for l in range(DEPTH):
        h = rms_norm(x, pre_mix_g[l])
        z = h @ w_in[l]
        q = z[..., :ATTN_WIDTH].reshape(B, S, N_HEADS, HEAD_DIM)
        k = z[..., ATTN_WIDTH:2 * ATTN_WIDTH].reshape(B, S, N_HEADS, HEAD_DIM)
        v = z[..., 2 * ATTN_WIDTH:3 * ATTN_WIDTH].reshape(B, S, N_HEADS, HEAD_DIM)
        u = z[..., 3 * ATTN_WIDTH:]
        a = chunked_attention(q, k, v, rel_bias[l]).reshape(B, S, ATTN_WIDTH)
        p = multiscale_pool(u, pool_w[l], pool_scale[l])
        m = jnp.concatenate([a, p], axis=-1) @ w_o[l]
        x = x + rms_norm(m, post_mix_g[l])
        h = rms_norm(x, pre_ffn_g[l])
        up = causal_dwconv(h @ w_up[l], conv_w[l], conv_b[l])
        y = (jax.nn.gelu(up[..., :D_FF], approximate=True) * up[..., D_FF:]) @ w_down[l]
        x = x + rms_norm(y, post_ffn_g[l])
    return x
```

```python
import math
from contextlib import ExitStack

import numpy as np
import concourse.bass as bass
import concourse.mybir as mybir
from concourse.bass_utils import run_bass_kernel_spmd

F32 = mybir.dt.float32
BF16 = mybir.dt.bfloat16
U8 = mybir.dt.uint8
AF = mybir.ActivationFunctionType
ALU = mybir.AluOpType

D = 2048
DC = 16
NH = 8
DFF = 5632
FC = 44
NT = 512
HALO_TILES = 3
HALO = HALO_TILES * NT
NCORES = 8
CPB = 4
EPS = 1e-6
NEG = -30000.0
QSCALE = 1.0 / math.sqrt(128.0)

P_WIN = 0
P_WO = 32
P_WUP = 48
P_WDN = 136
P_PW = 200
NPANEL = 201

ENGINES = ("pe", "act", "dve", "pool", "sp")
DMA_SLOTS = 8


class Instr:
    __slots__ = ("idx", "eng", "fn", "reads", "writes", "is_dma", "deps", "signal", "tick",
                 "dslot", "dval", "know", "waits")

    def __init__(self, idx, eng, fn, reads, writes, is_dma):
        self.idx = idx
        self.eng = eng
        self.fn = fn
        self.reads = reads
        self.writes = writes
        self.is_dma = is_dma
        self.deps = ()
        self.signal = False
        self.tick = 0
        self.dslot = None
        self.dval = 0
        self.know = None
        self.waits = ()


class Space:
    def __init__(self, ncells, cell):
        self.cell = cell
        self.lw = np.full(ncells, -1, dtype=np.int64)
        self.lr = {e: np.full(ncells, -1, dtype=np.int64) for e in ENGINES}
        self.lrd = {}


def _uniq(v, deps):
    m = int(v.max())
    if m < 0:
        return False
    if int(v.min()) == m:
        deps.add(m)
        return True
    for d in np.unique(v):
        if d >= 0:
            deps.add(int(d))
    return True


class Sched:
    def __init__(self):
        self.instrs = []
        self.spaces = {}
        self.res = {}

    def add_space(self, name, nbytes, cell):
        self.spaces[name] = Space((nbytes + cell - 1) // cell, cell)

    def op(self, eng, fn, reads=(), writes=(), dma=False):
        extra = [(r[0], 0, 2048) for r in reads if r[0].startswith("ps")]
        if extra:
            writes = tuple(writes) + tuple(extra)
        ins = Instr(len(self.instrs), eng, fn, tuple(reads), tuple(writes), dma)
        self._deps(ins)
        self.instrs.append(ins)
        return ins

    def _deps(self, ins):
        deps = set()
        eng = ins.eng
        instrs = self.instrs
        for acc in ins.reads:
            if acc[0] == "res":
                r = self.res.setdefault(acc[1], [-1, {}, []])
                if r[0] >= 0:
                    deps.add(r[0])
                continue
            sp = self.spaces[acc[0]]
            lo = acc[1] // sp.cell
            hi = (acc[2] + sp.cell - 1) // sp.cell
            _uniq(sp.lw[lo:hi], deps)
        for acc in ins.writes:
            if acc[0] == "res":
                r = self.res.setdefault(acc[1], [-1, {}, []])
                if r[0] >= 0:
                    deps.add(r[0])
                for d in r[1].values():
                    deps.add(d)
                for d in r[2]:
                    deps.add(d)
                continue
            sp = self.spaces[acc[0]]
            lo = acc[1] // sp.cell
            hi = (acc[2] + sp.cell - 1) // sp.cell
            _uniq(sp.lw[lo:hi], deps)
            for e in ENGINES:
                _uniq(sp.lr[e][lo:hi], deps)
            if sp.lrd:
                for c in range(lo, hi):
                    l = sp.lrd.get(c)
                    if l:
                        deps.update(l)
        for acc in ins.reads:
            if acc[0] == "res":
                r = self.res[acc[1]]
                if ins.is_dma:
                    r[2].append(ins.idx)
                else:
                    r[1][eng] = ins.idx
                continue
            sp = self.spaces[acc[0]]
            lo = acc[1] // sp.cell
            hi = (acc[2] + sp.cell - 1) // sp.cell
            if ins.is_dma:
                for c in range(lo, hi):
                    sp.lrd.setdefault(c, []).append(ins.idx)
            else:
                sp.lr[eng][lo:hi] = ins.idx
        for acc in ins.writes:
            if acc[0] == "res":
                r = self.res[acc[1]]
                r[0] = ins.idx
                r[1] = {}
                r[2] = []
                continue
            sp = self.spaces[acc[0]]
            lo = acc[1] // sp.cell
            hi = (acc[2] + sp.cell - 1) // sp.cell
            for e in ENGINES:
                sp.lr[e][lo:hi] = -1
            if sp.lrd:
                for c in range(lo, hi):
                    sp.lrd.pop(c, None)
            sp.lw[lo:hi] = ins.idx
        deps.discard(ins.idx)
        out = set()
        for d in deps:
            di = instrs[d]
            if di.eng == eng and not di.is_dma and not ins.is_dma and eng == "pe":
                continue
            out.add(d)
        best = {}
        keep = []
        for d in out:
            di = instrs[d]
            if di.is_dma:
                keep.append(d)
            elif best.get(di.eng, -1) < d:
                best[di.eng] = d
        keep.extend(best.values())
        ins.deps = tuple(sorted(keep))

    @staticmethod
    def _is_raw(writer, reader):
        for w in writer.writes:
            for r in reader.reads:
                if w[0] != r[0]:
                    continue
                if w[0] == "res":
                    if w[1] == r[1]:
                        return True
                elif w[1] < r[2] and r[1] < w[2]:
                    return True
        return False

    def finalize(self):
        instrs = self.instrs
        dma_count = {e: 0 for e in ENGINES}
        slot_last = {}
        slot_val = {}
        extra = {}
        for ins in instrs:
            if ins.is_dma:
                k = dma_count[ins.eng] % DMA_SLOTS
                dma_count[ins.eng] += 1
                key = (ins.eng, k)
                ins.dslot = key
                prev = slot_last.get(key)
                if prev is not None:
                    extra[ins.idx] = prev
                slot_last[key] = ins.idx
                slot_val[key] = slot_val.get(key, 0) + 16
                ins.dval = slot_val[key]
        for ins in instrs:
            ds = list(ins.deps)
            if ins.idx in extra and extra[ins.idx] not in ds:
                ds.append(extra[ins.idx])
            ins.deps = tuple(ds)
            for d in ds:
                if not instrs[d].is_dma:
                    instrs[d].signal = True
        ticks = {e: 0 for e in ENGINES}
        for ins in instrs:
            if not ins.is_dma and ins.signal:
                ticks[ins.eng] += 1
                ins.tick = ticks[ins.eng]
        know = {e: {} for e in ENGINES}
        nw = 0
        for ins in instrs:
            kn = know[ins.eng]
            need = {}
            for d in ins.deps:
                di = instrs[d]
                if di.is_dma:
                    key = ("dma",) + di.dslot
                    val = di.dval
                else:
                    key = ("eng", di.eng)
                    val = di.tick
                if kn.get(key, 0) >= val:
                    continue
                if need.get(key, 0) < val:
                    need[key] = val
            for key, val in need.items():
                kn[key] = val
            for d in ins.deps:
                di = instrs[d]
                if not di.is_dma and di.know is not None:
                    for k2, v2 in di.know.items():
                        if kn.get(k2, 0) < v2:
                            kn[k2] = v2
            ins.waits = tuple(need.items())
            nw += len(need)
            if not ins.is_dma and ins.signal:
                snap = dict(kn)
                snap[("eng", ins.eng)] = ins.tick
                ins.know = snap
        self.final_dma = dict(slot_val)
        self.stats = dict(n=len(instrs), waits=nw, ticks=dict(ticks), dmas=dict(dma_count))

    def emit(self, nc):
        per = {e: [] for e in ENGINES}
        for ins in self.instrs:
            per[ins.eng].append(ins)
        with ExitStack() as st:
            sems = {}
            for e in ENGINES:
                sems[("eng", e)] = st.enter_context(nc.semaphore("s_" + e))
            for key in self.final_dma:
                sems[("dma",) + key] = st.enter_context(nc.semaphore("d_%s%d" % key))
            block = st.enter_context(nc.Block())

            def stream(ename, final=False):
                def body(eng):
                    for ins in per[ename]:
                        for key, val in ins.waits:
                            eng.wait_ge(sems[key], val)
                        bi = ins.fn(eng)
                        if ins.is_dma:
                            bi.then_inc(sems[("dma",) + ins.dslot], 16)
                        elif ins.signal:
                            bi.then_inc(sems[("eng", ename)], 1)
                    if final:
                        for key, val in self.final_dma.items():
                            eng.wait_ge(sems[("dma",) + key], val)
                return body

            block.tensor(stream("pe"))
            block.scalar(stream("act"))
            block.vector(stream("dve"))
            block.gpsimd(stream("pool"))
            block.sync(stream("sp", final=True))


class V:
    def __init__(self, big, off, shape, dt):
        self.off = off
        self.shape = tuple(shape)
        self.es = 4 if dt == F32 else 2
        n = int(np.prod(shape))
        self.nbytes = n * self.es
        t = big[:, off:off + self.nbytes].bitcast(dt)
        if len(shape) == 2:
            t = t.rearrange("p (a b) -> p a b", a=shape[0])
        self.t = t
        self.row = shape[-1]

    def ap(self, a=None, lo=0, hi=None):
        if hi is None:
            hi = self.row
        if len(self.shape) == 2:
            b0 = self.off + (a * self.row + lo) * self.es
            b1 = self.off + (a * self.row + hi) * self.es
            return self.t[:, a, lo:hi], ("sb", b0, b1)
        b0 = self.off + lo * self.es
        b1 = self.off + hi * self.es
        return self.t[:, lo:hi], ("sb", b0, b1)

    def rng(self, a0, a1, lo=0, hi=None):
        if hi is None:
            hi = self.row
        b0 = self.off + (a0 * self.row + lo) * self.es
        b1 = self.off + ((a1 - 1) * self.row + hi) * self.es
        return self.t[:, a0:a1, lo:hi], ("sb", b0, b1)

    def whole(self):
        return ("sb", self.off, self.off + self.nbytes)


class Rot:
    def __init__(self, items):
        self.items = list(items)
        self.i = 0

    def next(self):
        it = self.items[self.i % len(self.items)]
        self.i += 1
        return it


def build_program(n_main):
    ntiles = HALO_TILES + n_main
    nc = bass.Bass("TRN2", target_bir_lowering=False)

    def din(name, shape, dt=F32):
        return nc.dram_tensor(name, list(shape), dt, kind="ExternalInput").ap()

    xin = din("xin", [ntiles, 128, DC * NT])
    yout = nc.dram_tensor("yout", [n_main, 128, DC * NT], F32, kind="ExternalOutput").ap()
    wsrc = din("wsrc", [2 * NPANEL, 128, 2048])
    gains_d = din("gains", [128, 2 * 4 * DC])
    convp_d = din("convp", [128, 2 * 88 * 4])
    pscale_d = din("pscale", [128, 2 * 8])
    relB_d = din("relB", [128, 2 * NH * 256])
    relc_d = din("relc", [128, 2 * NH])
    kmask_d = din("kmask", [128, 12])
    tokmask_d = din("tokmask", [128, HALO_TILES * NT])
    invc_d = din("invc", [128, 64])
    scr = nc.dram_tensor("wscr", [2 * NPANEL, 128, 2048], BF16, kind="Internal").ap()

    S = Sched()
    nbig = nc.sbuf_bytes_remaining // 32 * 32
    big = nc.alloc_sbuf_tensor("big", [128, nbig], U8)
    S.add_space("sb", nbig, 8)
    for b in range(8):
        S.add_space("ps%d" % b, 2048, 8)
    psum = [nc.alloc_psum_tensor("psb%d" % b, [128, 512], F32) for b in range(8)]
    psum_bf = [p[:].bitcast(BF16) for p in psum]

    off = [0]

    def alloc(shape, dt):
        v = V(big, off[0], shape, dt)
        off[0] += (v.nbytes + 31) // 32 * 32
        assert off[0] <= nbig, ("SBUF overflow", off[0], nbig)
        return v

    def alloc_at(o, shape, dt):
        return V(big, o, shape, dt)

    X = alloc([DC, NT], F32)
    KR = [alloc([NH, 2 * NT], BF16) for _ in range(2)]
    VR = [alloc([8, NH * 128], BF16) for _ in range(2)]
    R0 = off[0]
    off[0] += 77824
    G = alloc_at(R0, [FC, NT], BF16)
    A = alloc_at(R0 + 45056, [DC, NT], F32)
    Hn = alloc_at(R0 + 45056, [DC, NT], BF16)
    Q = alloc_at(R0, [NH, NT], BF16)
    U = alloc_at(R0 + 8192, [8, 16 + NT], F32)
    CAT = alloc_at(R0 + 25088, [DC, NT], BF16)
    PLD = alloc_at(R0 + 61440, [8, NT], BF16)
    PT = alloc_at(R0 + 69632, [7, NT], BF16)
    VT = alloc_at(R0 + 69632 + 7 * 1024, [1, NT], BF16)
    SQ = alloc([2, NT], BF16)
    TMP = alloc([4, 544], F32)
    WS = alloc([4, 2048], BF16)
    GAINS = alloc([2 * 4, DC], F32)
    CONVP = alloc([2 * 88, 4], F32)
    PSCALE = alloc([2, 8], F32)
    RELC = alloc([2, NH], F32)
    DREL = alloc_at(R0 + 45056, [NH, 256], BF16)
    BIAS12 = alloc([2 * 12, NH], F32)
    MHI = alloc([1, 64], BF16)
    IDN = alloc([1, 128], BF16)
    ONES = alloc([1, 128], BF16)
    KMASK = alloc([1, 12], F32)
    INVC = alloc([4, 16], F32)
    EPST = alloc([1, 8], F32)
    UCAR = alloc([2 * 8, 16], F32)
    CCAR = alloc([2 * 88, 2], F32)
    CFIX = alloc([88, 2], F32)
    CC2 = alloc([88, 2], F32)
    HCAR = alloc([DC, 2], BF16)
    TMASK = TMP

    def op(eng, fn, reads=(), writes=(), dma=False):
        return S.op(eng, fn, reads, writes, dma)

    import os as _os2
    ksub = int(_os2.environ.get("KSUB", "0"))

    def ps_acc(b, lo=0, hi=512):
        return ("ps%d" % b, lo * 4, hi * 4)

    def load_const(v, src):
        op("sp", lambda e: e.dma_start(out=v.t[:].rearrange("p a b -> p (a b)"), in_=src), writes=[v.whole()], dma=True)

    load_const(GAINS, gains_d)
    load_const(CONVP, convp_d)
    load_const(PSCALE, pscale_d)
    load_const(RELC, relc_d)
    load_const(KMASK, kmask_d)
    load_const(INVC, invc_d)
    op("dve", lambda e: e.memset(EPST.t[:], EPS), writes=[EPST.whole()])
    op("dve", lambda e: e.memset(ONES.t[:], 1.0), writes=[ONES.whole()])
    op("dve", lambda e: e.memset(MHI.t[:], 0.0), writes=[MHI.whole()])
    op("dve", lambda e: e.memset(MHI.t[0:64], NEG), writes=[MHI.whole()])
    for v in KR + VR + [UCAR, CCAR, VT, PT]:
        op("pool", lambda e, v=v: e.memset(v.t[:], 0.0), writes=[v.whole()])
    iden_d = din("iden", [128, 128])
    IDF = alloc_at(TMP.off, [1, 128], F32)
    op("sp", lambda e: e.dma_start(out=IDF.t[:, 0, :], in_=iden_d), writes=[IDF.whole()], dma=True)
    op("dve", lambda e: e.tensor_copy(out=IDN.t[:, 0, :], in_=IDF.t[:, 0, :]), reads=[IDF.whole()], writes=[IDN.whole()])
    KNEG = alloc_at(TMP.off + 2176, [1, 12], F32)
    op("dve", lambda e: e.tensor_scalar(out=KNEG.t[:, 0, :], in0=KMASK.t[:, 0, :], scalar1=-1.0, scalar2=-NEG,
                                        op0=ALU.add, op1=ALU.mult),
       reads=[KMASK.whole()], writes=[KNEG.whole()])
    for l_ in range(2):
        for lb in range(12):
            a, acc = BIAS12.ap(l_ * 12 + lb)
            op("dve", lambda e, a=a, lb=lb, l_=l_: e.tensor_scalar(out=a, in0=RELC.t[:, l_, :], scalar1=KNEG.t[:, 0, lb:lb + 1],
                                                                 scalar2=None, op0=ALU.add),
               reads=[KNEG.whole(), RELC.whole()], writes=[acc])

    converted = set()
    wrot = Rot(range(4))

    def load_panel(pid, width=2048):
        s = wrot.next()
        dst, acc = WS.ap(s, 0, width)
        op("sp", lambda e: e.dma_start(out=dst, in_=scr[pid, :, 0:width]), reads=[("res", "scr%d" % pid)],
           writes=[acc], dma=True)
        return s

    for l_ in range(2):
        base = l_ * NPANEL
        order = [(base + P_WIN + c, 2048) for c in list(range(8, 32)) + list(range(8))]
        order += [(base + P_WO + d_, 2048) for d_ in range(16)]
        order += [(base + P_PW, 2048)]
        for j_ in range(FC):
            order += [(base + P_WUP + j_, 2048), (base + P_WUP + j_ + FC, 2048)]
        order += [(base + P_WDN + k_, 11 * 128) for k_ in range(64)]
        for pid_, w_ in order:
            op("pool", lambda e, pid_=pid_, w_=w_: e.dma_start(out=scr[pid_, :, 0:w_], in_=wsrc[pid_, :, 0:w_]),
               writes=[("res", "scr%d" % pid_)], dma=True)

    gen_rot = Rot([6, 7])
    ffn_rot = Rot([0, 1, 4, 5, 6, 7])
    sq_rot = Rot([0, 1])
    tmp_rot = Rot([0, 1, 2, 3])
    pt_rot = Rot(range(7))
    st_rot = Rot([0, 1, 6, 7])

    def mm(out_b, lo, hi, lhsT, lacc, rhs, racc, start, stop, bf=False):
        tgt = psum[out_b][:, lo:hi]
        op("pe", lambda e: e.matmul(tgt, lhsT=lhsT, rhs=rhs, start=start, stop=stop, skip_group_check=True),
           reads=[lacc, racc], writes=[ps_acc(out_b, 0, 512)])

    def rstd_from_stats(bank, c0):
        ts = tmp_rot.next()
        r, racc = TMP.ap(ts, c0, NT)
        op("act", lambda e: e.activation(out=r, in_=psum[bank][:, c0:NT], func=AF.Sqrt, bias=EPST.t[:, 0, 0:1],
                                         scale=1.0 / D),
           reads=[ps_acc(bank, c0, NT), EPST.whole()], writes=[racc])
        op("dve", lambda e: e.reciprocal(out=r, in_=r), reads=[racc], writes=[racc])
        return ts

    def stats_chunk(bank, c0, src_ap, src_acc, c, last):
        sl = sq_rot.next()
        sq, sacc = SQ.ap(sl, c0, NT)
        op("act", lambda e: e.activation(out=sq, in_=src_ap, func=AF.Square), reads=[src_acc], writes=[sacc])
        o, oacc = ONES.ap(0)
        mm(bank, c0, NT, o, oacc, sq, sacc, start=(c == 0), stop=last)

    def pre_norm(l, n, c0):
        bank = 2
        for c in range(DC):
            xa, xacc = X.ap(c, c0, NT)
            stats_chunk(bank, c0, xa, xacc, c, c == DC - 1)
        ts = rstd_from_stats(bank, c0)
        r, racc = TMP.ap(ts, c0, NT)
        for c in range(DC):
            xa, xacc = X.ap(c, c0, NT)
            ha, hacc = Hn.ap(c, c0, NT)
            g = GAINS.t[:, l * 4 + n, c:c + 1]
            op("dve", lambda e, ha=ha, xa=xa, g=g: e.scalar_tensor_tensor(out=ha, in0=xa, scalar=g, in1=r,
                                                                        op0=ALU.mult, op1=ALU.mult),
               reads=[xacc, racc, GAINS.whole()], writes=[hacc])

    def post_norm_apply(l, n, c0, bank, mask_tile=None):
        ts = rstd_from_stats(bank, c0)
        r, racc = TMP.ap(ts, c0, NT)
        for c in range(DC):
            aa, aacc = A.ap(c, c0, NT)
            xa, xacc = X.ap(c, c0, NT)
            g = GAINS.t[:, l * 4 + n, c:c + 1]
            op("dve", lambda e, aa=aa: e.tensor_tensor(out=aa, in0=aa, in1=r, op=ALU.mult),
               reads=[aacc, racc], writes=[aacc])
            op("dve", lambda e, aa=aa, xa=xa, g=g: e.scalar_tensor_tensor(out=xa, in0=aa, scalar=g, in1=xa,
                                                                        op0=ALU.mult, op1=ALU.add),
               reads=[aacc, xacc, GAINS.whole()], writes=[xacc])
            if mask_tile is not None:
                ma, macc = mask_tile
                op("dve", lambda e, xa=xa, ma=ma: e.tensor_tensor(out=xa, in0=xa, in1=ma[:, c0:NT], op=ALU.mult),
                   reads=[xacc, macc], writes=[xacc])

    def mixer(l, ti, c0_kvu, c0_full):
        par = ti % 2
        c0 = c0_kvu
        pre_norm(l, 0, c0)
        for c in range(32):
            kind = c // 8
            j = c % 8
            if kind == 0 and c0_full is None:
                continue
            cc = c0_full if kind == 0 else c0
            s = load_panel(l * NPANEL + P_WIN + c)
            b = gen_rot.next()
            for kc in range(DC):
                w, wacc = WS.ap(s, kc * 128, (kc + 1) * 128)
                ha, hacc = Hn.ap(kc, cc, NT)
                mm(b, cc, NT, w, wacc, ha, hacc, start=(kc == 0), stop=(kc == DC - 1))
            src = psum[b][:, cc:NT]
            sacc = ps_acc(b, cc, NT)
            if kind == 0:
                qa, qacc = Q.ap(j, cc, NT)
                op("act", lambda e, qa=qa, src=src: e.activation(out=qa, in_=src, func=AF.Identity, scale=QSCALE),
                   reads=[sacc], writes=[qacc])
            elif kind == 1:
                ka, kacc = KR[l].ap(j, par * NT + cc, par * NT + NT)
                op("dve", lambda e, ka=ka, src=src: e.tensor_copy(out=ka, in_=src), reads=[sacc], writes=[kacc])
            elif kind == 2:
                va, vacc = VT.ap(0, cc, NT)
                tb0 = cc // 128
                if cc > tb0 * 128:
                    za, zacc = VT.ap(0, tb0 * 128, cc)
                    op("dve", lambda e, za=za: e.memset(za, 0.0), writes=[zacc])
                op("act", lambda e, va=va, src=src: e.activation(out=va, in_=src, func=AF.Identity),
                   reads=[sacc], writes=[vacc])
                bt = gen_rot.next()
                for tb in range(tb0, 4):
                    ia, iacc = IDN.ap(0)
                    vin, vinacc = VT.ap(0, tb * 128, (tb + 1) * 128)
                    tgt = psum_bf[bt][:, tb * 128:(tb + 1) * 128]
                    op("pe", lambda e, tgt=tgt, vin=vin, ia=ia: e.transpose(tgt, vin, ia),
                       reads=[vinacc, iacc], writes=[("ps%d" % bt, 0, 2048)])
                dst = VR[l].t[:, par * 4 + tb0:par * 4 + 4, j * 128:(j + 1) * 128]
                dacc = ("sb", VR[l].off + ((par * 4 + tb0) * 1024 + j * 128) * 2,
                        VR[l].off + ((par * 4 + 3) * 1024 + (j + 1) * 128) * 2)
                srcb = psum_bf[bt][:, tb0 * 128:512].rearrange("p (a b) -> p a b", b=128)
                op("dve", lambda e, dst=dst, srcb=srcb: e.tensor_copy(out=dst, in_=srcb),
                   reads=[("ps%d" % bt, tb0 * 256, 1024)], writes=[dacc])
            else:
                ua, uacc = U.ap(j, 16 + cc, 16 + NT)
                op("act", lambda e, ua=ua, src=src: e.activation(out=ua, in_=src, func=AF.Identity),
                   reads=[sacc], writes=[uacc])
        for j in range(8):
            ua, uacc = U.ap(j, 0, 16)
            ca, cacc = UCAR.ap(l * 8 + j)
            op("dve", lambda e, ua=ua, ca=ca: e.tensor_copy(out=ua, in_=ca), reads=[cacc], writes=[uacc])
        if c0_full is not None:
            cf = c0_full
            special = (ti == HALO_TILES)
            for j in range(8):
                g = j // 2
                w = 2 << g
                cur_ap = lambda lo, hi, j=j: U.ap(j, lo, hi)
                src_t = None
                lo = 16 + cf
                hi = 16 + NT
                prev = None
                span = 1
                for step in range(g + 1):
                    ts = tmp_rot.next()
                    sh = span
                    o_lo = lo - (w - 2 * span)
                    oa, oacc = TMP.ap(ts, o_lo, hi)
                    if prev is None:
                        a0, a0acc = U.ap(j, o_lo, hi)
                        a1, a1acc = U.ap(j, o_lo - sh, hi - sh)
                    else:
                        a0, a0acc = TMP.ap(prev, o_lo, hi)
                        a1, a1acc = TMP.ap(prev, o_lo - sh, hi - sh)
                    op("dve", lambda e, oa=oa, a0=a0, a1=a1: e.tensor_tensor(out=oa, in0=a0, in1=a1, op=ALU.add),
                       reads=[a0acc, a1acc], writes=[oacc])
                    prev = ts
                    span *= 2
                sa, sacc = TMP.ap(prev, lo, hi)
                ua, uacc = U.ap(j, lo, hi)
                pa, pacc = PLD.ap(j, cf, NT)
                op("dve", lambda e, pa=pa, sa=sa, ua=ua, w=w: e.scalar_tensor_tensor(out=pa, in0=sa, scalar=1.0 / w, in1=ua,
                                                                                 op0=ALU.mult, op1=ALU.subtract),
                   reads=[sacc, uacc], writes=[pacc])
                if special and cf == 0:
                    s16, s16acc = TMP.ap(prev, 16, 32)
                    u16, u16acc = U.ap(j, 16, 32)
                    p16, p16acc = PLD.ap(j, 0, 16)
                    ic, icacc = INVC.ap(g)
                    op("dve", lambda e, s16=s16, ic=ic: e.tensor_tensor(out=s16, in0=s16, in1=ic, op=ALU.mult),
                       reads=[s16acc, icacc], writes=[s16acc])
                    op("dve", lambda e, p16=p16, s16=s16, u16=u16: e.tensor_tensor(out=p16, in0=s16, in1=u16, op=ALU.subtract),
                       reads=[s16acc, u16acc], writes=[p16acc])
        for j in range(8):
            ua, uacc = U.ap(j, NT, NT + 16)
            ca, cacc = UCAR.ap(l * 8 + j)
            op("dve", lambda e, ua=ua, ca=ca: e.tensor_copy(out=ca, in_=ua), reads=[uacc], writes=[cacc])
        if c0_full is None:
            return
        cf = c0_full
        qc0 = cf // 64
        for h in range(NH):
            ts = tmp_rot.next()
            ba, bacc = TMP.ap(ts, 0, 256)
            op("sp", lambda e, ba=ba, h=h: e.dma_start(out=ba, in_=relB_d[:, (l * NH + h) * 256:(l * NH + h + 1) * 256]),
               writes=[bacc], dma=True)
            da, dacc = DREL.ap(h)
            op("dve", lambda e, da=da, ba=ba, h=h: e.tensor_scalar(out=da, in0=ba, scalar1=RELC.t[:, l, h:h + 1],
                                                                 scalar2=None, op0=ALU.subtract),
               reads=[bacc, RELC.whole()], writes=[dacc])
        if ksub == 1:
            return
        items = []
        for h in range(NH):
            blks = []
            for i in (3, 4, 5, 6, 7, 2, 1, 0):
                cb = 2 * i - 8
                qa_c = max(cb, 0, qc0)
                qb_c = min(cb + 10, 8)
                if qa_c < qb_c:
                    blks.append((i, cb, qa_c, qb_c))
            for bi, (i, cb, qa_c, qb_c) in enumerate(blks):
                items.append(dict(h=h, i=i, cb=cb, qa_c=qa_c, qb_c=qb_c, first=(bi == 0), last=(bi == len(blks) - 1)))

        def att_st(it):
            h, i, cb, qa_c, qb_c = it["h"], it["i"], it["cb"], it["qa_c"], it["qb_c"]
            qa, qb = qa_c * 64, qb_c * 64
            if i < 4:
                kpar, kblk = 1 - par, i
            else:
                kpar, kblk = par, i - 4
            it["kpar"], it["kblk"], it["qa"], it["qb"] = kpar, kblk, qa, qb
            kT, kacc = KR[l].ap(h, kpar * NT + kblk * 128, kpar * NT + (kblk + 1) * 128)
            qT, qacc = Q.ap(h, qa, qb)
            sb_ = st_rot.next()
            it["sb"] = sb_
            mm(sb_, qa, qb, kT, kacc, qT, qacc, start=True, stop=False)
            ia, iacc = IDN.ap(0)
            ra = max(qa_c, cb)
            rb = min(qb_c, cb + 4)
            if ra < rb:
                da, dacc = DREL.ap(h, (ra - cb) * 64, (rb - cb) * 64)
                mm(sb_, ra * 64, rb * 64, ia, iacc, da, dacc, start=False, stop=False)
            if cb + 9 < qb_c and cb + 9 >= qa_c:
                ma, macc = MHI.ap(0)
                mm(sb_, (cb + 9) * 64, (cb + 10) * 64, ia, iacc, ma, macc, start=False, stop=False)

        def att_ex(it):
            h, i, qa, qb, sb_ = it["h"], it["i"], it["qa"], it["qb"], it["sb"]
            psl = pt_rot.next()
            pT, pacc = PT.ap(psl, qa, qb)
            it["pT"], it["pacc"] = pT, pacc
            lb = (ti - 1) * 4 + i if i < 4 else ti * 4 + (i - 4)
            if 0 <= lb < 12:
                bias = BIAS12.t[:, l * 12 + lb, h:h + 1]
            else:
                bias = RELC.t[:, l, h:h + 1]
            op("act", lambda e: e.activation(out=pT, in_=psum[sb_][:, qa:qb], func=AF.Exp, bias=bias, scale=1.0),
               reads=[ps_acc(sb_, qa, qb), RELC.whole(), BIAS12.whole()], writes=[pacc])

        def att_pv(it):
            h, qa, qb, pT, pacc = it["h"], it["qa"], it["qb"], it["pT"], it["pacc"]
            nb, db = (2, 3) if h % 2 == 0 else (4, 5)
            vblk = it["kpar"] * 4 + it["kblk"]
            va = VR[l].t[:, vblk, h * 128:(h + 1) * 128]
            vacc = ("sb", VR[l].off + (vblk * 1024 + h * 128) * 2, VR[l].off + (vblk * 1024 + (h + 1) * 128) * 2)
            mm(nb, qa, qb, va, vacc, pT, pacc, start=it["first"], stop=False)
            dl, dlacc = ONES.ap(0)
            mm(db, qa, qb, dl, dlacc, pT, pacc, start=it["first"], stop=False)
            if not it["last"]:
                return
            ts = tmp_rot.next()
            rd, rdacc = TMP.ap(ts, cf, NT)
            if ti < HALO_TILES:
                op("dve", lambda e: e.tensor_scalar(out=rd, in0=psum[db][:, cf:NT], scalar1=1e-30, scalar2=None, op0=ALU.max),
                   reads=[ps_acc(db, cf, NT)], writes=[rdacc])
                op("dve", lambda e: e.reciprocal(out=rd, in_=rd), reads=[rdacc], writes=[rdacc])
            else:
                op("dve", lambda e: e.reciprocal(out=rd, in_=psum[db][:, cf:NT]), reads=[ps_acc(db, cf, NT)], writes=[rdacc])
            ca, cacc = CAT.ap(h, cf, NT)
            op("dve", lambda e: e.tensor_tensor(out=ca, in0=psum[nb][:, cf:NT], in1=rd, op=ALU.mult),
               reads=[ps_acc(nb, cf, NT), rdacc], writes=[cacc])

        LOOK = 3
        for k_ in range(min(LOOK, len(items))):
            att_st(items[k_])
        for k_, it in enumerate(items):
            if k_ + LOOK < len(items):
                att_st(items[k_ + LOOK])
            att_ex(it)
            att_pv(it)
        if ksub == 2:
            return
        s = load_panel(l * NPANEL + P_PW)
        for g in range(4):
            for oc in range(2):
                b = gen_rot.next()
                for kc in range(2):
                    o = ((g * 2 + oc) * 2 + kc) * 128
                    w, wacc = WS.ap(s, o, o + 128)
                    pa, pacc = PLD.ap(g * 2 + kc, cf, NT)
                    mm(b, cf, NT, w, wacc, pa, pacc, start=(kc == 0), stop=(kc == 1))
                ca, cacc = CAT.ap(8 + g * 2 + oc, cf, NT)
                sc = PSCALE.t[:, l, g * 2 + oc:g * 2 + oc + 1]
                op("act", lambda e, ca=ca, b=b, sc=sc: e.activation(out=ca, in_=psum[b][:, cf:NT], func=AF.Identity, scale=sc),
                   reads=[ps_acc(b, cf, NT), PSCALE.whole()], writes=[cacc])
        if ksub == 3:
            return
        bank = 2
        for d in range(DC):
            s = load_panel(l * NPANEL + P_WO + d)
            b = gen_rot.next()
            for kc in range(DC):
                w, wacc = WS.ap(s, kc * 128, (kc + 1) * 128)
                ca, cacc = CAT.ap(kc, cf, NT)
                mm(b, cf, NT, w, wacc, ca, cacc, start=(kc == 0), stop=(kc == DC - 1))
            aa, aacc = A.ap(d, cf, NT)
            op("dve", lambda e, aa=aa, b=b: e.tensor_copy(out=aa, in_=psum[b][:, cf:NT]),
               reads=[ps_acc(b, cf, NT)], writes=[aacc])
            stats_chunk(bank, cf, psum[b][:, cf:NT], ps_acc(b, cf, NT), d, d == DC - 1)
        post_norm_apply(l, 1, cf, bank)

    def ffn_carry_prep(l, ti, c0):
        pre_norm(l, 2, c0)
        op("dve", lambda e: e.tensor_copy(out=HCAR.t[:, :, :], in_=Hn.t[:, :, NT - 2:NT]),
           reads=[Hn.whole()], writes=[HCAR.whole()])

    def ffn(l, ti, c0, mask_tile=None, from_hcar=False):
        pre_norm(l, 2, c0)
        n = NT - c0
        cw = lambda c, k: CONVP.t[:, l * 88 + c, k:k + 1]
        CW3 = CONVP.t[:, l * 88:(l + 1) * 88, :]
        car = CCAR.t[:, l * 88:(l + 1) * 88, :]
        caracc = ("sb", CCAR.off + l * 88 * 8, CCAR.off + (l + 1) * 88 * 8)
        fx = CFIX.t
        if not from_hcar:
            op("dve", lambda e: e.tensor_tensor(out=fx[:, :, 1:2], in0=car[:, :, 1:2], in1=CW3[:, :, 0:1], op=ALU.mult),
               reads=[caracc, CONVP.whole()], writes=[CFIX.whole()])
            op("dve", lambda e: e.tensor_tensor(out=fx[:, :, 0:1], in0=car[:, :, 0:1], in1=CW3[:, :, 0:1], op=ALU.mult),
               reads=[caracc, CONVP.whole()], writes=[CFIX.whole()])
            ts0 = tmp_rot.next()
            t88 = TMP.t[:, ts0, 0:88]
            t88acc = TMP.ap(ts0, 0, 88)[1]
            op("dve", lambda e: e.tensor_tensor(out=t88.unsqueeze(2), in0=car[:, :, 1:2], in1=CW3[:, :, 1:2], op=ALU.mult),
               reads=[caracc, CONVP.whole()], writes=[t88acc])
            op("dve", lambda e: e.tensor_tensor(out=fx[:, :, 0:1], in0=fx[:, :, 0:1], in1=t88.unsqueeze(2), op=ALU.add),
               reads=[t88acc, CFIX.whole()], writes=[CFIX.whole()])
        for j in range(FC):
            accs = []
            for half in range(2):
                c = j + half * FC
                s = load_panel(l * NPANEL + P_WUP + c)
                b = ffn_rot.next()
                for kc in range(DC):
                    w, wacc = WS.ap(s, kc * 128, (kc + 1) * 128)
                    ha, hacc = Hn.ap(kc, c0, NT)
                    mm(b, c0, NT, w, wacc, ha, hacc, start=(kc == 0), stop=(kc == DC - 1))
                if from_hcar:
                    for kc in range(DC):
                        w, wacc = WS.ap(s, kc * 128, (kc + 1) * 128)
                        hc, hcacc = HCAR.ap(kc)
                        mm(3, 0, 2, w, wacc, hc, hcacc, start=(kc == 0), stop=(kc == DC - 1))
                    c2, c2acc = CC2.ap(c)
                    fxc, fxacc = CFIX.ap(c)
                    op("act", lambda e, c2=c2: e.activation(out=c2, in_=psum[3][:, 0:2], func=AF.Identity),
                       reads=[ps_acc(3, 0, 2)], writes=[c2acc])
                    op("dve", lambda e, c2=c2, fxc=fxc, c=c: e.tensor_scalar(out=fxc, in0=c2, scalar1=cw(c, 0), scalar2=None,
                                                                          op0=ALU.mult),
                       reads=[c2acc, CONVP.whole()], writes=[fxacc])
                    op("dve", lambda e, c2=c2, fxc=fxc, c=c: e.scalar_tensor_tensor(out=fxc[:, 0:1], in0=c2[:, 1:2], scalar=cw(c, 1),
                                                                                  in1=fxc[:, 0:1], op0=ALU.mult, op1=ALU.add),
                       reads=[c2acc, fxacc, CONVP.whole()], writes=[fxacc])
                ts = tmp_rot.next()
                a_, aacc = TMP.ap(ts, c0, NT)
                P = psum[b]
                pacc = ps_acc(b, c0, NT)
                op("act", lambda e, a_=a_, P=P, c=c: e.activation(out=a_, in_=P[:, c0:NT], func=AF.Identity,
                                                                 bias=cw(c, 3), scale=cw(c, 2)),
                   reads=[pacc, CONVP.whole()], writes=[aacc])
                cc_, ccacc = CCAR.ap(l * 88 + c)
                op("act", lambda e, cc_=cc_, P=P: e.activation(out=cc_, in_=P[:, NT - 2:NT], func=AF.Identity),
                   reads=[pacc], writes=[ccacc])
                a1, a1acc = TMP.ap(ts, c0 + 1, NT)
                op("dve", lambda e, a1=a1, P=P, c=c: e.scalar_tensor_tensor(out=a1, in0=P[:, c0:NT - 1], scalar=cw(c, 1), in1=a1,
                                                                          op0=ALU.mult, op1=ALU.add),
                   reads=[pacc, aacc, CONVP.whole()], writes=[aacc])
                a2, a2acc = TMP.ap(ts, c0 + 2, NT)
                op("dve", lambda e, a2=a2, P=P, c=c: e.scalar_tensor_tensor(out=a2, in0=P[:, c0:NT - 2], scalar=cw(c, 0), in1=a2,
                                                                          op0=ALU.mult, op1=ALU.add),
                   reads=[pacc, aacc, CONVP.whole()], writes=[aacc])
                f2, f2acc = TMP.ap(ts, c0, c0 + 2)
                op("dve", lambda e, f2=f2, c=c: e.tensor_tensor(out=f2, in0=f2, in1=fx[:, c, :], op=ALU.add),
                   reads=[aacc, CFIX.whole()], writes=[aacc])
                accs.append((ts, a_, aacc))
            (tsa, aa, aaacc), (tsb, ab, abacc) = accs
            op("act", lambda e, aa=aa: e.activation(out=aa, in_=aa, func=AF.Gelu_apprx_tanh), reads=[aaacc], writes=[aaacc])
            ga, gacc = G.ap(j, c0, NT)
            op("dve", lambda e, ga=ga, aa=aa, ab=ab: e.tensor_tensor(out=ga, in0=aa, in1=ab, op=ALU.mult),
               reads=[aaacc, abacc], writes=[gacc])
        bank = 2
        for d in range(DC):
            b = ffn_rot.next()
            if b == bank:
                b = ffn_rot.next()
            for piece in range(4):
                s = load_panel(l * NPANEL + P_WDN + d * 4 + piece, width=11 * 128)
                for kk in range(11):
                    w, wacc = WS.ap(s, kk * 128, (kk + 1) * 128)
                    ga, gacc = G.ap(piece * 11 + kk, c0, NT)
                    mm(b, c0, NT, w, wacc, ga, gacc, start=(piece == 0 and kk == 0), stop=(piece == 3 and kk == 10))
            aa, aacc = A.ap(d, c0, NT)
            op("dve", lambda e, aa=aa, b=b: e.tensor_copy(out=aa, in_=psum[b][:, c0:NT]),
               reads=[ps_acc(b, c0, NT)], writes=[aacc])
            stats_chunk(bank, c0, psum[b][:, c0:NT], ps_acc(b, c0, NT), d, d == DC - 1)
        post_norm_apply(l, 3, c0, bank, mask_tile)

    import os as _os
    kstage = int(_os.environ.get("KSTAGE", "100000"))
    cnt = [0]

    def stage(fn, *a):
        cnt[0] += 1
        if cnt[0] <= kstage:
            fn(*a)

    for ti in range(ntiles):
        for c_ in range(DC):
            xa_, xacc_ = X.ap(c_)
            op("sp", lambda e, ti=ti, c_=c_, xa_=xa_: e.dma_start(out=xa_, in_=xin[ti, :, c_ * NT:(c_ + 1) * NT]),
               writes=[xacc_], dma=True)
        if ti == 0:
            stage(mixer, 0, ti, 384, None)
            continue
        mask_tile = None
        if ti < HALO_TILES:
            ma, macc = TMP.ap(3, 0, NT)
            op("sp", lambda e, ma=ma, ti=ti: e.dma_start(out=ma, in_=tokmask_d[:, ti * NT:(ti + 1) * NT]),
               writes=[macc], dma=True)
            mask_tile = (ma, macc)
            tmp_rot.items = [0, 1, 2]
        if ti == 1:
            stage(mixer, 0, ti, 0, 384)
            stage(ffn, 0, ti, 384, mask_tile)
            stage(mixer, 1, ti, 448, None)
        elif ti == 2:
            stage(mixer, 0, ti, 0, 0)
            stage(ffn, 0, ti, 0, mask_tile)
            stage(mixer, 1, ti, 0, 448)
            stage(ffn_carry_prep, 1, ti, 448)
        else:
            stage(mixer, 0, ti, 0, 0)
            stage(ffn, 0, ti, 0)
            stage(mixer, 1, ti, 0, 0)
            stage(ffn, 1, ti, 0, None, ti == HALO_TILES)
            for c_ in range(DC):
                xa_, xacc_ = X.ap(c_)
                op("sp", lambda e, ti=ti, c_=c_, xa_=xa_: e.dma_start(out=yout[ti - HALO_TILES, :, c_ * NT:(c_ + 1) * NT], in_=xa_),
                   reads=[xacc_], dma=True)
        if ti < HALO_TILES:
            tmp_rot.items = [0, 1, 2, 3]

    S.finalize()
    S.emit(nc)
    return nc, S.stats


def _fm(v):
    v = np.asarray(v, dtype=np.float32)
    lead = v.shape[:-1]
    n = v.shape[-1] // 128
    v = v.reshape(lead + (n, 128))
    return np.ascontiguousarray(np.moveaxis(v, -1, 0))


def _panelize(w, kc_per_panel=None):
    K, N = w.shape
    a = w.reshape(K // 128, 128, N // 128, 128)
    a = a.transpose(2, 1, 0, 3)
    return np.ascontiguousarray(a.reshape(N // 128, 128, (K // 128) * 128))


def prepare_shared(inp):
    L = 2
    wsrc = np.zeros((L * NPANEL, 128, 2048), dtype=np.float32)
    for l in range(L):
        base = l * NPANEL
        wsrc[base + P_WIN:base + P_WIN + 32] = _panelize(np.asarray(inp["w_in"][l], dtype=np.float32))
        wsrc[base + P_WO:base + P_WO + 16] = _panelize(np.asarray(inp["w_o"][l], dtype=np.float32))
        wsrc[base + P_WUP:base + P_WUP + 88] = _panelize(np.asarray(inp["w_up"][l], dtype=np.float32))
        wd = _panelize(np.asarray(inp["w_down"][l], dtype=np.float32))
        wd = wd.reshape(16, 128, 4, 11 * 128).transpose(0, 2, 1, 3).reshape(64, 128, 11 * 128)
        wsrc[base + P_WDN:base + P_WDN + 64, :, :11 * 128] = wd
        pw = np.asarray(inp["pool_w"][l], dtype=np.float32)
        pw = pw.reshape(4, 2, 128, 2, 128).transpose(2, 0, 3, 1, 4)
        wsrc[base + P_PW] = pw.reshape(128, 2048)
    gains = np.stack([np.stack([_fm(inp[k][l]) for k in ("pre_mix_g", "post_mix_g", "pre_ffn_g", "post_ffn_g")], axis=1)
                      for l in range(L)], axis=1)
    gains = np.ascontiguousarray(gains.reshape(128, L * 4 * DC))
    cw = np.asarray(inp["conv_w"], dtype=np.float32)
    cb = np.asarray(inp["conv_b"], dtype=np.float32)
    cp = np.concatenate([cw, cb[:, None, :]], axis=1)
    cp = cp.reshape(L, 4, 88, 128).transpose(3, 0, 2, 1)
    convp = np.ascontiguousarray(cp.reshape(128, L * 88 * 4))
    ps = np.asarray(inp["pool_scale"], dtype=np.float32).reshape(L, 8, 128).transpose(2, 0, 1)
    pscale = np.ascontiguousarray(ps.reshape(128, L * 8))
    rb = np.asarray(inp["rel_bias"], dtype=np.float32)
    k = np.arange(128)[:, None]
    q = np.arange(256)[None, :]
    idx = np.clip(q - k, -128, 128) + 128
    relB = rb[:, :, idx]
    relB = np.array(relB.transpose(2, 0, 1, 3))
    relB[64:, :, :, 0:64] = NEG
    relB = np.ascontiguousarray(relB.reshape(128, L * NH * 256))
    relc = np.ascontiguousarray(np.broadcast_to(rb[:, :, 256].reshape(1, L * NH), (128, L * NH)))
    iden = np.eye(128, dtype=np.float32)
    return dict(wsrc=wsrc, gains=gains, convp=convp, pscale=pscale, relB=relB, relc=relc, iden=iden)


def prepare_core(x, core, s_core, n_main):
    ntiles = HALO_TILES + n_main
    b = core // CPB
    t0 = (core % CPB) * s_core
    lo = t0 - HALO
    ntok = ntiles * NT
    xs = np.zeros((ntok, D), dtype=np.float32)
    a = max(lo, 0)
    xs[a - lo:] = x[b, a:t0 + s_core]
    xin = xs.reshape(ntiles, NT, DC, 128).transpose(0, 3, 2, 1)
    xin = np.ascontiguousarray(xin.reshape(ntiles, 128, DC * NT))
    pos = lo + np.arange(HALO)
    valid = (pos >= 0).astype(np.float32)
    tokmask = np.ascontiguousarray(np.broadcast_to(valid[None, :], (128, HALO)))
    kmask = np.ascontiguousarray(valid.reshape(12, 128).T)
    invc = np.zeros((128, 4, 16), dtype=np.float32)
    for g, w in enumerate((2, 4, 8, 16)):
        p = t0 + np.arange(16)
        invc[:, g, :] = (1.0 / np.minimum(p + 1, w))[None, :]
    return dict(xin=xin, tokmask=tokmask, kmask=kmask, invc=np.ascontiguousarray(invc.reshape(128, 64)))


_PROG = {}


def run(inputs, s_core, trace=False):
    n_main = s_core // NT
    if n_main not in _PROG:
        _PROG[n_main] = build_program(n_main)
    nc, stats = _PROG[n_main]
    shared = prepare_shared(inputs)
    x = np.asarray(inputs["x"], dtype=np.float32)
    in_maps = []
    for c in range(NCORES):
        m = dict(shared)
        m.update(prepare_core(x, c, s_core, n_main))
        in_maps.append(m)
    res = run_bass_kernel_spmd(nc, in_maps, core_ids=list(range(NCORES)), trace=trace)
    B = x.shape[0]
    out = np.empty((B, CPB * s_core, D), dtype=np.float32)
    for c in range(NCORES):
        y = np.asarray(res.results[c]["yout"]).reshape(n_main, 128, DC, NT)
        y = y.transpose(0, 3, 2, 1).reshape(n_main * NT, D)
        b = c // CPB
        t0 = (c % CPB) * s_core
        out[b, t0:t0 + s_core] = y
    return out, res


def kernel(**inputs):
    out, _ = run(inputs, 4096)
    return out
```

```python
import math
from contextlib import ExitStack

import numpy as np
import concourse.bass as bass
import concourse.mybir as mybir
from concourse.bass_utils import run_bass_kernel_spmd

F32 = mybir.dt.float32
BF16 = mybir.dt.bfloat16
U8 = mybir.dt.uint8
AF = mybir.ActivationFunctionType
ALU = mybir.AluOpType

D = 2048
DC = 16
NH = 8
DFF = 5632
FC = 44
NT = 512
HALO_TILES = 3
HALO = HALO_TILES * NT
NCORES = 8
CPB = 4
EPS = 1e-6
NEG = -30000.0
QSCALE = 1.0 / math.sqrt(128.0)

P_WIN = 0
P_WO = 32
P_WUP = 48
P_WDN = 136
P_PW = 200
NPANEL = 201

ENGINES = ("pe", "act", "dve", "pool", "sp")
DMA_SLOTS = 8


class Instr:
    __slots__ = ("idx", "eng", "fn", "reads", "writes", "is_dma", "deps", "signal", "tick",
                 "dslot", "dval", "know", "waits")

    def __init__(self, idx, eng, fn, reads, writes, is_dma):
        self.idx = idx
        self.eng = eng
        self.fn = fn
        self.reads = reads
        self.writes = writes
        self.is_dma = is_dma
        self.deps = ()
        self.signal = False
        self.tick = 0
        self.dslot = None
        self.dval = 0
        self.know = None
        self.waits = ()


class Space:
    def __init__(self, ncells, cell):
        self.cell = cell
        self.lw = np.full(ncells, -1, dtype=np.int64)
        self.lr = {e: np.full(ncells, -1, dtype=np.int64) for e in ENGINES}
        self.lrd = {}


def _uniq(v, deps):
    m = int(v.max())
    if m < 0:
        return False
    if int(v.min()) == m:
        deps.add(m)
        return True
    for d in np.unique(v):
        if d >= 0:
            deps.add(int(d))
    return True


class Sched:
    def __init__(self):
        self.instrs = []
        self.spaces = {}
        self.res = {}

    def add_space(self, name, nbytes, cell):
        self.spaces[name] = Space((nbytes + cell - 1) // cell, cell)

    def op(self, eng, fn, reads=(), writes=(), dma=False):
        extra = [(r[0], 0, 2048) for r in reads if r[0].startswith("ps")]
        if extra:
            writes = tuple(writes) + tuple(extra)
        ins = Instr(len(self.instrs), eng, fn, tuple(reads), tuple(writes), dma)
        self._deps(ins)
        self.instrs.append(ins)
        return ins

    def _deps(self, ins):
        deps = set()
        eng = ins.eng
        instrs = self.instrs
        for acc in ins.reads:
            if acc[0] == "res":
                r = self.res.setdefault(acc[1], [-1, {}, []])
                if r[0] >= 0:
                    deps.add(r[0])
                continue
            sp = self.spaces[acc[0]]
            lo = acc[1] // sp.cell
            hi = (acc[2] + sp.cell - 1) // sp.cell
            _uniq(sp.lw[lo:hi], deps)
        for acc in ins.writes:
            if acc[0] == "res":
                r = self.res.setdefault(acc[1], [-1, {}, []])
                if r[0] >= 0:
                    deps.add(r[0])
                for d in r[1].values():
                    deps.add(d)
                for d in r[2]:
                    deps.add(d)
                continue
            sp = self.spaces[acc[0]]
            lo = acc[1] // sp.cell
            hi = (acc[2] + sp.cell - 1) // sp.cell
            _uniq(sp.lw[lo:hi], deps)
            for e in ENGINES:
                _uniq(sp.lr[e][lo:hi], deps)
            if sp.lrd:
                for c in range(lo, hi):
                    l = sp.lrd.get(c)
                    if l:
                        deps.update(l)
        for acc in ins.reads:
            if acc[0] == "res":
                r = self.res[acc[1]]
                if ins.is_dma:
                    r[2].append(ins.idx)
                else:
                    r[1][eng] = ins.idx
                continue
            sp = self.spaces[acc[0]]
            lo = acc[1] // sp.cell
            hi = (acc[2] + sp.cell - 1) // sp.cell
            if ins.is_dma:
                for c in range(lo, hi):
                    sp.lrd.setdefault(c, []).append(ins.idx)
            else:
                sp.lr[eng][lo:hi] = ins.idx
        for acc in ins.writes:
            if acc[0] == "res":
                r = self.res[acc[1]]
                r[0] = ins.idx
                r[1] = {}
                r[2] = []
                continue
            sp = self.spaces[acc[0]]
            lo = acc[1] // sp.cell
            hi = (acc[2] + sp.cell - 1) // sp.cell
            for e in ENGINES:
                sp.lr[e][lo:hi] = -1
            if sp.lrd:
                for c in range(lo, hi):
                    sp.lrd.pop(c, None)
            sp.lw[lo:hi] = ins.idx
        deps.discard(ins.idx)
        out = set()
        for d in deps:
            di = instrs[d]
            if di.eng == eng and not di.is_dma and not ins.is_dma and eng == "pe":
                continue
            out.add(d)
        best = {}
        keep = []
        for d in out:
            di = instrs[d]
            if di.is_dma:
                keep.append(d)
            elif best.get(di.eng, -1) < d:
                best[di.eng] = d
        keep.extend(best.values())
        ins.deps = tuple(sorted(keep))

    @staticmethod
    def _is_raw(writer, reader):
        for w in writer.writes:
            for r in reader.reads:
                if w[0] != r[0]:
                    continue
                if w[0] == "res":
                    if w[1] == r[1]:
                        return True
                elif w[1] < r[2] and r[1] < w[2]:
                    return True
        return False

    def finalize(self):
        instrs = self.instrs
        dma_count = {e: 0 for e in ENGINES}
        slot_last = {}
        slot_val = {}
        extra = {}
        for ins in instrs:
            if ins.is_dma:
                k = dma_count[ins.eng] % DMA_SLOTS
                dma_count[ins.eng] += 1
                key = (ins.eng, k)
                ins.dslot = key
                prev = slot_last.get(key)
                if prev is not None:
                    extra[ins.idx] = prev
                slot_last[key] = ins.idx
                slot_val[key] = slot_val.get(key, 0) + 16
                ins.dval = slot_val[key]
        for ins in instrs:
            ds = list(ins.deps)
            if ins.idx in extra and extra[ins.idx] not in ds:
                ds.append(extra[ins.idx])
            ins.deps = tuple(ds)
            for d in ds:
                if not instrs[d].is_dma:
                    instrs[d].signal = True
        ticks = {e: 0 for e in ENGINES}
        for ins in instrs:
            if not ins.is_dma and ins.signal:
                ticks[ins.eng] += 1
                ins.tick = ticks[ins.eng]
        know = {e: {} for e in ENGINES}
        nw = 0
        for ins in instrs:
            kn = know[ins.eng]
            need = {}
            for d in ins.deps:
                di = instrs[d]
                if di.is_dma:
                    key = ("dma",) + di.dslot
                    val = di.dval
                else:
                    key = ("eng", di.eng)
                    val = di.tick
                if kn.get(key, 0) >= val:
                    continue
                if need.get(key, 0) < val:
                    need[key] = val
            for key, val in need.items():
                kn[key] = val
            for d in ins.deps:
                di = instrs[d]
                if not di.is_dma and di.know is not None:
                    for k2, v2 in di.know.items():
                        if kn.get(k2, 0) < v2:
                            kn[k2] = v2
            ins.waits = tuple(need.items())
            nw += len(need)
            if not ins.is_dma and ins.signal:
                snap = dict(kn)
                snap[("eng", ins.eng)] = ins.tick
                ins.know = snap
        self.final_dma = dict(slot_val)
        self.stats = dict(n=len(instrs), waits=nw, ticks=dict(ticks), dmas=dict(dma_count))

    def emit(self, nc):
        per = {e: [] for e in ENGINES}
        for ins in self.instrs:
            per[ins.eng].append(ins)
        with ExitStack() as st:
            sems = {}
            for e in ENGINES:
                sems[("eng", e)] = st.enter_context(nc.semaphore("s_" + e))
            for key in self.final_dma:
                sems[("dma",) + key] = st.enter_context(nc.semaphore("d_%s%d" % key))
            block = st.enter_context(nc.Block())

            def stream(ename, final=False):
                def body(eng):
                    for ins in per[ename]:
                        for key, val in ins.waits:
                            eng.wait_ge(sems[key], val)
                        bi = ins.fn(eng)
                        if ins.is_dma:
                            bi.then_inc(sems[("dma",) + ins.dslot], 16)
                        elif ins.signal:
                            bi.then_inc(sems[("eng", ename)], 1)
                    if final:
                        for key, val in self.final_dma.items():
                            eng.wait_ge(sems[("dma",) + key], val)
                return body

            block.tensor(stream("pe"))
            block.scalar(stream("act"))
            block.vector(stream("dve"))
            block.gpsimd(stream("pool"))
            block.sync(stream("sp", final=True))


class V:
    def __init__(self, big, off, shape, dt):
        self.off = off
        self.shape = tuple(shape)
        self.es = 4 if dt == F32 else 2
        n = int(np.prod(shape))
        self.nbytes = n * self.es
        t = big[:, off:off + self.nbytes].bitcast(dt)
        if len(shape) == 2:
            t = t.rearrange("p (a b) -> p a b", a=shape[0])
        self.t = t
        self.row = shape[-1]

    def ap(self, a=None, lo=0, hi=None):
        if hi is None:
            hi = self.row
        if len(self.shape) == 2:
            b0 = self.off + (a * self.row + lo) * self.es
            b1 = self.off + (a * self.row + hi) * self.es
            return self.t[:, a, lo:hi], ("sb", b0, b1)
        b0 = self.off + lo * self.es
        b1 = self.off + hi * self.es
        return self.t[:, lo:hi], ("sb", b0, b1)

    def rng(self, a0, a1, lo=0, hi=None):
        if hi is None:
            hi = self.row
        b0 = self.off + (a0 * self.row + lo) * self.es
        b1 = self.off + ((a1 - 1) * self.row + hi) * self.es
        return self.t[:, a0:a1, lo:hi], ("sb", b0, b1)

    def whole(self):
        return ("sb", self.off, self.off + self.nbytes)


class Rot:
    def __init__(self, items):
        self.items = list(items)
        self.i = 0

    def next(self):
        it = self.items[self.i % len(self.items)]
        self.i += 1
        return it


def build_program(n_main):
    ntiles = HALO_TILES + n_main
    nc = bass.Bass("TRN2", target_bir_lowering=False)

    def din(name, shape, dt=F32):
        return nc.dram_tensor(name, list(shape), dt, kind="ExternalInput").ap()

    xin = din("xin", [ntiles, 128, DC * NT])
    yout = nc.dram_tensor("yout", [n_main, 128, DC * NT], F32, kind="ExternalOutput").ap()
    wsrc = din("wsrc", [2 * NPANEL, 128, 2048])
    gains_d = din("gains", [128, 2 * 4 * DC])
    convp_d = din("convp", [128, 2 * 88 * 4])
    pscale_d = din("pscale", [128, 2 * 8])
    relB_d = din("relB", [128, 2 * NH * 256])
    relc_d = din("relc", [128, 2 * NH])
    kmask_d = din("kmask", [128, 12])
    tokmask_d = din("tokmask", [128, HALO_TILES * NT])
    invc_d = din("invc", [128, 64])
    scr = nc.dram_tensor("wscr", [2 * NPANEL, 128, 2048], BF16, kind="Internal").ap()

    S = Sched()
    nbig = nc.sbuf_bytes_remaining // 32 * 32
    big = nc.alloc_sbuf_tensor("big", [128, nbig], U8)
    S.add_space("sb", nbig, 8)
    for b in range(8):
        S.add_space("ps%d" % b, 2048, 8)
    psum = [nc.alloc_psum_tensor("psb%d" % b, [128, 512], F32) for b in range(8)]
    psum_bf = [p[:].bitcast(BF16) for p in psum]

    off = [0]

    def alloc(shape, dt):
        v = V(big, off[0], shape, dt)
        off[0] += (v.nbytes + 31) // 32 * 32
        assert off[0] <= nbig, ("SBUF overflow", off[0], nbig)
        return v

    def alloc_at(o, shape, dt):
        return V(big, o, shape, dt)

    X = alloc([DC, NT], F32)
    KR = [alloc([NH, 2 * NT], BF16) for _ in range(2)]
    VR = [alloc([8, NH * 128], BF16) for _ in range(2)]
    R0 = off[0]
    off[0] += 77824
    G = alloc_at(R0, [FC, NT], BF16)
    A = alloc_at(R0 + 45056, [DC, NT], F32)
    Hn = alloc_at(R0 + 45056, [DC, NT], BF16)
    Q = alloc_at(R0, [NH, NT], BF16)
    U = alloc_at(R0 + 8192, [8, 16 + NT], F32)
    CAT = alloc_at(R0 + 25088, [DC, NT], BF16)
    PLD = alloc_at(R0 + 61440, [8, NT], BF16)
    PT = alloc_at(R0 + 69632, [7, NT], BF16)
    VT = alloc_at(R0 + 69632 + 7 * 1024, [1, NT], BF16)
    SQ = alloc([2, NT], BF16)
    TMP = alloc([4, 544], F32)
    WS = alloc([4, 2048], BF16)
    GAINS = alloc([2 * 4, DC], F32)
    CONVP = alloc([2 * 88, 4], F32)
    PSCALE = alloc([2, 8], F32)
    RELC = alloc([2, NH], F32)
    DREL = alloc_at(R0 + 45056, [NH, 256], BF16)
    BIAS12 = alloc([2 * 12, NH], F32)
    MHI = alloc([1, 64], BF16)
    IDN = alloc([1, 128], BF16)
    ONES = alloc([1, 128], BF16)
    KMASK = alloc([1, 12], F32)
    INVC = alloc([4, 16], F32)
    EPST = alloc([1, 8], F32)
    UCAR = alloc([2 * 8, 16], F32)
    CCAR = alloc([2 * 88, 2], F32)
    CFIX = alloc([88, 2], F32)
    CC2 = alloc([88, 2], F32)
    HCAR = alloc([DC, 2], BF16)
    TMASK = TMP

    def op(eng, fn, reads=(), writes=(), dma=False):
        return S.op(eng, fn, reads, writes, dma)

    import os as _os2
    ksub = int(_os2.environ.get("KSUB", "0"))

    def ps_acc(b, lo=0, hi=512):
        return ("ps%d" % b, lo * 4, hi * 4)

    def load_const(v, src):
        op("sp", lambda e: e.dma_start(out=v.t[:].rearrange("p a b -> p (a b)"), in_=src), writes=[v.whole()], dma=True)

    load_const(GAINS, gains_d)
    load_const(CONVP, convp_d)
    load_const(PSCALE, pscale_d)
    load_const(RELC, relc_d)
    load_const(KMASK, kmask_d)
    load_const(INVC, invc_d)
    op("dve", lambda e: e.memset(EPST.t[:], EPS), writes=[EPST.whole()])
    op("dve", lambda e: e.memset(ONES.t[:], 1.0), writes=[ONES.whole()])
    op("dve", lambda e: e.memset(MHI.t[:], 0.0), writes=[MHI.whole()])
    op("dve", lambda e: e.memset(MHI.t[0:64], NEG), writes=[MHI.whole()])
    for v in KR + VR + [UCAR, CCAR, VT, PT]:
        op("pool", lambda e, v=v: e.memset(v.t[:], 0.0), writes=[v.whole()])
    iden_d = din("iden", [128, 128])
    IDF = alloc_at(TMP.off, [1, 128], F32)
    op("sp", lambda e: e.dma_start(out=IDF.t[:, 0, :], in_=iden_d), writes=[IDF.whole()], dma=True)
    op("dve", lambda e: e.tensor_copy(out=IDN.t[:, 0, :], in_=IDF.t[:, 0, :]), reads=[IDF.whole()], writes=[IDN.whole()])
    KNEG = alloc_at(TMP.off + 2176, [1, 12], F32)
    op("dve", lambda e: e.tensor_scalar(out=KNEG.t[:, 0, :], in0=KMASK.t[:, 0, :], scalar1=-1.0, scalar2=-NEG,
                                        op0=ALU.add, op1=ALU.mult),
       reads=[KMASK.whole()], writes=[KNEG.whole()])
    for l_ in range(2):
        for lb in range(12):
            a, acc = BIAS12.ap(l_ * 12 + lb)
            op("dve", lambda e, a=a, lb=lb, l_=l_: e.tensor_scalar(out=a, in0=RELC.t[:, l_, :], scalar1=KNEG.t[:, 0, lb:lb + 1],
                                                                 scalar2=None, op0=ALU.add),
               reads=[KNEG.whole(), RELC.whole()], writes=[acc])

    converted = set()
    wrot = Rot(range(4))

    def load_panel(pid, width=2048):
        s = wrot.next()
        dst, acc = WS.ap(s, 0, width)
        op("sp", lambda e: e.dma_start(out=dst, in_=scr[pid, :, 0:width]), reads=[("res", "scr%d" % pid)],
           writes=[acc], dma=True)
        return s

    for l_ in range(2):
        base = l_ * NPANEL
        order = [(base + P_WIN + c, 2048) for c in list(range(8, 32)) + list(range(8))]
        order += [(base + P_WO + d_, 2048) for d_ in range(16)]
        order += [(base + P_PW, 2048)]
        for j_ in range(FC):
            order += [(base + P_WUP + j_, 2048), (base + P_WUP + j_ + FC, 2048)]
        order += [(base + P_WDN + k_, 11 * 128) for k_ in range(64)]
        for pid_, w_ in order:
            op("pool", lambda e, pid_=pid_, w_=w_: e.dma_start(out=scr[pid_, :, 0:w_], in_=wsrc[pid_, :, 0:w_]),
               writes=[("res", "scr%d" % pid_)], dma=True)

    gen_rot = Rot([6, 7])
    ffn_rot = Rot([0, 1, 4, 5, 6, 7])
    sq_rot = Rot([0, 1])
    tmp_rot = Rot([0, 1, 2, 3])
    pt_rot = Rot(range(7))
    st_rot = Rot([0, 1, 6, 7])

    def mm(out_b, lo, hi, lhsT, lacc, rhs, racc, start, stop, bf=False):
        tgt = psum[out_b][:, lo:hi]
        op("pe", lambda e: e.matmul(tgt, lhsT=lhsT, rhs=rhs, start=start, stop=stop, skip_group_check=True),
           reads=[lacc, racc], writes=[ps_acc(out_b, 0, 512)])

    def rstd_from_stats(bank, c0):
        ts = tmp_rot.next()
        r, racc = TMP.ap(ts, c0, NT)
        op("act", lambda e: e.activation(out=r, in_=psum[bank][:, c0:NT], func=AF.Sqrt, bias=EPST.t[:, 0, 0:1],
                                         scale=1.0 / D),
           reads=[ps_acc(bank, c0, NT), EPST.whole()], writes=[racc])
        op("dve", lambda e: e.reciprocal(out=r, in_=r), reads=[racc], writes=[racc])
        return ts

    def stats_chunk(bank, c0, src_ap, src_acc, c, last):
        sl = sq_rot.next()
        sq, sacc = SQ.ap(sl, c0, NT)
        op("act", lambda e: e.activation(out=sq, in_=src_ap, func=AF.Square), reads=[src_acc], writes=[sacc])
        o, oacc = ONES.ap(0)
        mm(bank, c0, NT, o, oacc, sq, sacc, start=(c == 0), stop=last)

    def pre_norm(l, n, c0):
        bank = 2
        for c in range(DC):
            xa, xacc = X.ap(c, c0, NT)
            stats_chunk(bank, c0, xa, xacc, c, c == DC - 1)
        ts = rstd_from_stats(bank, c0)
        r, racc = TMP.ap(ts, c0, NT)
        for c in range(DC):
            xa, xacc = X.ap(c, c0, NT)
            ha, hacc = Hn.ap(c, c0, NT)
            g = GAINS.t[:, l * 4 + n, c:c + 1]
            op("dve", lambda e, ha=ha, xa=xa, g=g: e.scalar_tensor_tensor(out=ha, in0=xa, scalar=g, in1=r,
                                                                        op0=ALU.mult, op1=ALU.mult),
               reads=[xacc, racc, GAINS.whole()], writes=[hacc])

    def post_norm_apply(l, n, c0, bank, mask_tile=None):
        ts = rstd_from_stats(bank, c0)
        r, racc = TMP.ap(ts, c0, NT)
        for c in range(DC):
            aa, aacc = A.ap(c, c0, NT)
            xa, xacc = X.ap(c, c0, NT)
            g = GAINS.t[:, l * 4 + n, c:c + 1]
            op("dve", lambda e, aa=aa: e.tensor_tensor(out=aa, in0=aa, in1=r, op=ALU.mult),
               reads=[aacc, racc], writes=[aacc])
            op("dve", lambda e, aa=aa, xa=xa, g=g: e.scalar_tensor_tensor(out=xa, in0=aa, scalar=g, in1=xa,
                                                                        op0=ALU.mult, op1=ALU.add),
               reads=[aacc, xacc, GAINS.whole()], writes=[xacc])
            if mask_tile is not None:
                ma, macc = mask_tile
                op("dve", lambda e, xa=xa, ma=ma: e.tensor_tensor(out=xa, in0=xa, in1=ma[:, c0:NT], op=ALU.mult),
                   reads=[xacc, macc], writes=[xacc])

    def mixer(l, ti, c0_kvu, c0_full):
        par = ti % 2
        c0 = c0_kvu
        pre_norm(l, 0, c0)
        for c in range(32):
            kind = c // 8
            j = c % 8
            if kind == 0 and c0_full is None:
                continue
            cc = c0_full if kind == 0 else c0
            s = load_panel(l * NPANEL + P_WIN + c)
            b = gen_rot.next()
            for kc in range(DC):
                w, wacc = WS.ap(s, kc * 128, (kc + 1) * 128)
                ha, hacc = Hn.ap(kc, cc, NT)
                mm(b, cc, NT, w, wacc, ha, hacc, start=(kc == 0), stop=(kc == DC - 1))
            src = psum[b][:, cc:NT]
            sacc = ps_acc(b, cc, NT)
            if kind == 0:
                qa, qacc = Q.ap(j, cc, NT)
                op("act", lambda e, qa=qa, src=src: e.activation(out=qa, in_=src, func=AF.Identity, scale=QSCALE),
                   reads=[sacc], writes=[qacc])
            elif kind == 1:
                ka, kacc = KR[l].ap(j, par * NT + cc, par * NT + NT)
                op("dve", lambda e, ka=ka, src=src: e.tensor_copy(out=ka, in_=src), reads=[sacc], writes=[kacc])
            elif kind == 2:
                va, vacc = VT.ap(0, cc, NT)
                tb0 = cc // 128
                if cc > tb0 * 128:
                    za, zacc = VT.ap(0, tb0 * 128, cc)
                    op("dve", lambda e, za=za: e.memset(za, 0.0), writes=[zacc])
                op("act", lambda e, va=va, src=src: e.activation(out=va, in_=src, func=AF.Identity),
                   reads=[sacc], writes=[vacc])
                bt = gen_rot.next()
                for tb in range(tb0, 4):
                    ia, iacc = IDN.ap(0)
                    vin, vinacc = VT.ap(0, tb * 128, (tb + 1) * 128)
                    tgt = psum_bf[bt][:, tb * 128:(tb + 1) * 128]
                    op("pe", lambda e, tgt=tgt, vin=vin, ia=ia: e.transpose(tgt, vin, ia),
                       reads=[vinacc, iacc], writes=[("ps%d" % bt, 0, 2048)])
                dst = VR[l].t[:, par * 4 + tb0:par * 4 + 4, j * 128:(j + 1) * 128]
                dacc = ("sb", VR[l].off + ((par * 4 + tb0) * 1024 + j * 128) * 2,
                        VR[l].off + ((par * 4 + 3) * 1024 + (j + 1) * 128) * 2)
                srcb = psum_bf[bt][:, tb0 * 128:512].rearrange("p (a b) -> p a b", b=128)
                op("dve", lambda e, dst=dst, srcb=srcb: e.tensor_copy(out=dst, in_=srcb),
                   reads=[("ps%d" % bt, tb0 * 256, 1024)], writes=[dacc])
            else:
                ua, uacc = U.ap(j, 16 + cc, 16 + NT)
                op("act", lambda e, ua=ua, src=src: e.activation(out=ua, in_=src, func=AF.Identity),
                   reads=[sacc], writes=[uacc])
        for j in range(8):
            ua, uacc = U.ap(j, 0, 16)
            ca, cacc = UCAR.ap(l * 8 + j)
            op("dve", lambda e, ua=ua, ca=ca: e.tensor_copy(out=ua, in_=ca), reads=[cacc], writes=[uacc])
        if c0_full is not None:
            cf = c0_full
            special = (ti == HALO_TILES)
            for j in range(8):
                g = j // 2
                w = 2 << g
                cur_ap = lambda lo, hi, j=j: U.ap(j, lo, hi)
                src_t = None
                lo = 16 + cf
                hi = 16 + NT
                prev = None
                span = 1
                for step in range(g + 1):
                    ts = tmp_rot.next()
                    sh = span
                    o_lo = lo - (w - 2 * span)
                    oa, oacc = TMP.ap(ts, o_lo, hi)
                    if prev is None:
                        a0, a0acc = U.ap(j, o_lo, hi)
                        a1, a1acc = U.ap(j, o_lo - sh, hi - sh)
                    else:
                        a0, a0acc = TMP.ap(prev, o_lo, hi)
                        a1, a1acc = TMP.ap(prev, o_lo - sh, hi - sh)
                    op("dve", lambda e, oa=oa, a0=a0, a1=a1: e.tensor_tensor(out=oa, in0=a0, in1=a1, op=ALU.add),
                       reads=[a0acc, a1acc], writes=[oacc])
                    prev = ts
                    span *= 2
                sa, sacc = TMP.ap(prev, lo, hi)
                ua, uacc = U.ap(j, lo, hi)
                pa, pacc = PLD.ap(j, cf, NT)
                op("dve", lambda e, pa=pa, sa=sa, ua=ua, w=w: e.scalar_tensor_tensor(out=pa, in0=sa, scalar=1.0 / w, in1=ua,
                                                                                 op0=ALU.mult, op1=ALU.subtract),
                   reads=[sacc, uacc], writes=[pacc])
                if special and cf == 0:
                    s16, s16acc = TMP.ap(prev, 16, 32)
                    u16, u16acc = U.ap(j, 16, 32)
                    p16, p16acc = PLD.ap(j, 0, 16)
                    ic, icacc = INVC.ap(g)
                    op("dve", lambda e, s16=s16, ic=ic: e.tensor_tensor(out=s16, in0=s16, in1=ic, op=ALU.mult),
                       reads=[s16acc, icacc], writes=[s16acc])
                    op("dve", lambda e, p16=p16, s16=s16, u16=u16: e.tensor_tensor(out=p16, in0=s16, in1=u16, op=ALU.subtract),
                       reads=[s16acc, u16acc], writes=[p16acc])
        for j in range(8):
            ua, uacc = U.ap(j, NT, NT + 16)
            ca, cacc = UCAR.ap(l * 8 + j)
            op("dve", lambda e, ua=ua, ca=ca: e.tensor_copy(out=ca, in_=ua), reads=[uacc], writes=[cacc])
        if c0_full is None:
            return
        cf = c0_full
        qc0 = cf // 64
        for h in range(NH):
            ts = tmp_rot.next()
            ba, bacc = TMP.ap(ts, 0, 256)
            op("sp", lambda e, ba=ba, h=h: e.dma_start(out=ba, in_=relB_d[:, (l * NH + h) * 256:(l * NH + h + 1) * 256]),
               writes=[bacc], dma=True)
            da, dacc = DREL.ap(h)
            op("dve", lambda e, da=da, ba=ba, h=h: e.tensor_scalar(out=da, in0=ba, scalar1=RELC.t[:, l, h:h + 1],
                                                                 scalar2=None, op0=ALU.subtract),
               reads=[bacc, RELC.whole()], writes=[dacc])
        if ksub == 1:
            return
        items = []
        for h in range(NH):
            blks = []
            for i in (3, 4, 5, 6, 7, 2, 1, 0):
                cb = 2 * i - 8
                qa_c = max(cb, 0, qc0)
                qb_c = min(cb + 10, 8)
                if qa_c < qb_c:
                    blks.append((i, cb, qa_c, qb_c))
            for bi, (i, cb, qa_c, qb_c) in enumerate(blks):
                items.append(dict(h=h, i=i, cb=cb, qa_c=qa_c, qb_c=qb_c, first=(bi == 0), last=(bi == len(blks) - 1)))

        def att_st(it):
            h, i, cb, qa_c, qb_c = it["h"], it["i"], it["cb"], it["qa_c"], it["qb_c"]
            qa, qb = qa_c * 64, qb_c * 64
            if i < 4:
                kpar, kblk = 1 - par, i
            else:
                kpar, kblk = par, i - 4
            it["kpar"], it["kblk"], it["qa"], it["qb"] = kpar, kblk, qa, qb
            kT, kacc = KR[l].ap(h, kpar * NT + kblk * 128, kpar * NT + (kblk + 1) * 128)
            qT, qacc = Q.ap(h, qa, qb)
            sb_ = st_rot.next()
            it["sb"] = sb_
            mm(sb_, qa, qb, kT, kacc, qT, qacc, start=True, stop=False)
            ia, iacc = IDN.ap(0)
            ra = max(qa_c, cb)
            rb = min(qb_c, cb + 4)
            if ra < rb:
                da, dacc = DREL.ap(h, (ra - cb) * 64, (rb - cb) * 64)
                mm(sb_, ra * 64, rb * 64, ia, iacc, da, dacc, start=False, stop=False)
            if cb + 9 < qb_c and cb + 9 >= qa_c:
                ma, macc = MHI.ap(0)
                mm(sb_, (cb + 9) * 64, (cb + 10) * 64, ia, iacc, ma, macc, start=False, stop=False)

        def att_ex(it):
            h, i, qa, qb, sb_ = it["h"], it["i"], it["qa"], it["qb"], it["sb"]
            psl = pt_rot.next()
            pT, pacc = PT.ap(psl, qa, qb)
            it["pT"], it["pacc"] = pT, pacc
            lb = (ti - 1) * 4 + i if i < 4 else ti * 4 + (i - 4)
            if 0 <= lb < 12:
                bias = BIAS12.t[:, l * 12 + lb, h:h + 1]
            else:
                bias = RELC.t[:, l, h:h + 1]
            op("act", lambda e: e.activation(out=pT, in_=psum[sb_][:, qa:qb], func=AF.Exp, bias=bias, scale=1.0),
               reads=[ps_acc(sb_, qa, qb), RELC.whole(), BIAS12.whole()], writes=[pacc])

        def att_pv(it):
            h, qa, qb, pT, pacc = it["h"], it["qa"], it["qb"], it["pT"], it["pacc"]
            nb, db = (2, 3) if h % 2 == 0 else (4, 5)
            vblk = it["kpar"] * 4 + it["kblk"]
            va = VR[l].t[:, vblk, h * 128:(h + 1) * 128]
            vacc = ("sb", VR[l].off + (vblk * 1024 + h * 128) * 2, VR[l].off + (vblk * 1024 + (h + 1) * 128) * 2)
            mm(nb, qa, qb, va, vacc, pT, pacc, start=it["first"], stop=False)
            dl, dlacc = ONES.ap(0)
            mm(db, qa, qb, dl, dlacc, pT, pacc, start=it["first"], stop=False)
            if not it["last"]:
                return
            ts = tmp_rot.next()
            rd, rdacc = TMP.ap(ts, cf, NT)
            if ti < HALO_TILES:
                op("dve", lambda e: e.tensor_scalar(out=rd, in0=psum[db][:, cf:NT], scalar1=1e-30, scalar2=None, op0=ALU.max),
                   reads=[ps_acc(db, cf, NT)], writes=[rdacc])
                op("dve", lambda e: e.reciprocal(out=rd, in_=rd), reads=[rdacc], writes=[rdacc])
            else:
                op("dve", lambda e: e.reciprocal(out=rd, in_=psum[db][:, cf:NT]), reads=[ps_acc(db, cf, NT)], writes=[rdacc])
            ca, cacc = CAT.ap(h, cf, NT)
            op("dve", lambda e: e.tensor_tensor(out=ca, in0=psum[nb][:, cf:NT], in1=rd, op=ALU.mult),
               reads=[ps_acc(nb, cf, NT), rdacc], writes=[cacc])

        LOOK = 3
        for k_ in range(min(LOOK, len(items))):
            att_st(items[k_])
        for k_, it in enumerate(items):
            if k_ + LOOK < len(items):
                att_st(items[k_ + LOOK])
            att_ex(it)
            att_pv(it)
        if ksub == 2:
            return
        s = load_panel(l * NPANEL + P_PW)
        for g in range(4):
            for oc in range(2):
                b = gen_rot.next()
                for kc in range(2):
                    o = ((g * 2 + oc) * 2 + kc) * 128
                    w, wacc = WS.ap(s, o, o + 128)
                    pa, pacc = PLD.ap(g * 2 + kc, cf, NT)
                    mm(b, cf, NT, w, wacc, pa, pacc, start=(kc == 0), stop=(kc == 1))
                ca, cacc = CAT.ap(8 + g * 2 + oc, cf, NT)
                sc = PSCALE.t[:, l, g * 2 + oc:g * 2 + oc + 1]
                op("act", lambda e, ca=ca, b=b, sc=sc: e.activation(out=ca, in_=psum[b][:, cf:NT], func=AF.Identity, scale=sc),
                   reads=[ps_acc(b, cf, NT), PSCALE.whole()], writes=[cacc])
        if ksub == 3:
            return
        bank = 2
        for d in range(DC):
            s = load_panel(l * NPANEL + P_WO + d)
            b = gen_rot.next()
            for kc in range(DC):
                w, wacc = WS.ap(s, kc * 128, (kc + 1) * 128)
                ca, cacc = CAT.ap(kc, cf, NT)
                mm(b, cf, NT, w, wacc, ca, cacc, start=(kc == 0), stop=(kc == DC - 1))
            aa, aacc = A.ap(d, cf, NT)
            op("dve", lambda e, aa=aa, b=b: e.tensor_copy(out=aa, in_=psum[b][:, cf:NT]),
               reads=[ps_acc(b, cf, NT)], writes=[aacc])
            stats_chunk(bank, cf, psum[b][:, cf:NT], ps_acc(b, cf, NT), d, d == DC - 1)
        post_norm_apply(l, 1, cf, bank)

    def ffn_carry_prep(l, ti, c0):
        pre_norm(l, 2, c0)
        op("dve", lambda e: e.tensor_copy(out=HCAR.t[:, :, :], in_=Hn.t[:, :, NT - 2:NT]),
           reads=[Hn.whole()], writes=[HCAR.whole()])

    def ffn(l, ti, c0, mask_tile=None, from_hcar=False):
        pre_norm(l, 2, c0)
        n = NT - c0
        cw = lambda c, k: CONVP.t[:, l * 88 + c, k:k + 1]
        CW3 = CONVP.t[:, l * 88:(l + 1) * 88, :]
        car = CCAR.t[:, l * 88:(l + 1) * 88, :]
        caracc = ("sb", CCAR.off + l * 88 * 8, CCAR.off + (l + 1) * 88 * 8)
        fx = CFIX.t
        if not from_hcar:
            op("dve", lambda e: e.tensor_tensor(out=fx[:, :, 1:2], in0=car[:, :, 1:2], in1=CW3[:, :, 0:1], op=ALU.mult),
               reads=[caracc, CONVP.whole()], writes=[CFIX.whole()])
            op("dve", lambda e: e.tensor_tensor(out=fx[:, :, 0:1], in0=car[:, :, 0:1], in1=CW3[:, :, 0:1], op=ALU.mult),
               reads=[caracc, CONVP.whole()], writes=[CFIX.whole()])
            ts0 = tmp_rot.next()
            t88 = TMP.t[:, ts0, 0:88]
            t88acc = TMP.ap(ts0, 0, 88)[1]
            op("dve", lambda e: e.tensor_tensor(out=t88.unsqueeze(2), in0=car[:, :, 1:2], in1=CW3[:, :, 1:2], op=ALU.mult),
               reads=[caracc, CONVP.whole()], writes=[t88acc])
            op("dve", lambda e: e.tensor_tensor(out=fx[:, :, 0:1], in0=fx[:, :, 0:1], in1=t88.unsqueeze(2), op=ALU.add),
               reads=[t88acc, CFIX.whole()], writes=[CFIX.whole()])
        for j in range(FC):
            accs = []
            for half in range(2):
                c = j + half * FC
                s = load_panel(l * NPANEL + P_WUP + c)
                b = ffn_rot.next()
                for kc in range(DC):
                    w, wacc = WS.ap(s, kc * 128, (kc + 1) * 128)
                    ha, hacc = Hn.ap(kc, c0, NT)
                    mm(b, c0, NT, w, wacc, ha, hacc, start=(kc == 0), stop=(kc == DC - 1))
                if from_hcar:
                    for kc in range(DC):
                        w, wacc = WS.ap(s, kc * 128, (kc + 1) * 128)
                        hc, hcacc = HCAR.ap(kc)
                        mm(3, 0, 2, w, wacc, hc, hcacc, start=(kc == 0), stop=(kc == DC - 1))
                    c2, c2acc = CC2.ap(c)
                    fxc, fxacc = CFIX.ap(c)
                    op("act", lambda e, c2=c2: e.activation(out=c2, in_=psum[3][:, 0:2], func=AF.Identity),
                       reads=[ps_acc(3, 0, 2)], writes=[c2acc])
                    op("dve", lambda e, c2=c2, fxc=fxc, c=c: e.tensor_scalar(out=fxc, in0=c2, scalar1=cw(c, 0), scalar2=None,
                                                                          op0=ALU.mult),
                       reads=[c2acc, CONVP.whole()], writes=[fxacc])
                    op("dve", lambda e, c2=c2, fxc=fxc, c=c: e.scalar_tensor_tensor(out=fxc[:, 0:1], in0=c2[:, 1:2], scalar=cw(c, 1),
                                                                                  in1=fxc[:, 0:1], op0=ALU.mult, op1=ALU.add),
                       reads=[c2acc, fxacc, CONVP.whole()], writes=[fxacc])
                ts = tmp_rot.next()
                a_, aacc = TMP.ap(ts, c0, NT)
                P = psum[b]
                pacc = ps_acc(b, c0, NT)
                op("act", lambda e, a_=a_, P=P, c=c: e.activation(out=a_, in_=P[:, c0:NT], func=AF.Identity,
                                                                 bias=cw(c, 3), scale=cw(c, 2)),
                   reads=[pacc, CONVP.whole()], writes=[aacc])
                cc_, ccacc = CCAR.ap(l * 88 + c)
                op("act", lambda e, cc_=cc_, P=P: e.activation(out=cc_, in_=P[:, NT - 2:NT], func=AF.Identity),
                   reads=[pacc], writes=[ccacc])
                a1, a1acc = TMP.ap(ts, c0 + 1, NT)
                op("dve", lambda e, a1=a1, P=P, c=c: e.scalar_tensor_tensor(out=a1, in0=P[:, c0:NT - 1], scalar=cw(c, 1), in1=a1,
                                                                          op0=ALU.mult, op1=ALU.add),
                   reads=[pacc, aacc, CONVP.whole()], writes=[aacc])
                a2, a2acc = TMP.ap(ts, c0 + 2, NT)
                op("dve", lambda e, a2=a2, P=P, c=c: e.scalar_tensor_tensor(out=a2, in0=P[:, c0:NT - 2], scalar=cw(c, 0), in1=a2,
                                                                          op0=ALU.mult, op1=ALU.add),
                   reads=[pacc, aacc, CONVP.whole()], writes=[aacc])
                f2, f2acc = TMP.ap(ts, c0, c0 + 2)
                op("dve", lambda e, f2=f2, c=c: e.tensor_tensor(out=f2, in0=f2, in1=fx[:, c, :], op=ALU.add),
                   reads=[aacc, CFIX.whole()], writes=[aacc])
                accs.append((ts, a_, aacc))
            (tsa, aa, aaacc), (tsb, ab, abacc) = accs
            op("act", lambda e, aa=aa: e.activation(out=aa, in_=aa, func=AF.Gelu_apprx_tanh), reads=[aaacc], writes=[aaacc])
            ga, gacc = G.ap(j, c0, NT)
            op("dve", lambda e, ga=ga, aa=aa, ab=ab: e.tensor_tensor(out=ga, in0=aa, in1=ab, op=ALU.mult),
               reads=[aaacc, abacc], writes=[gacc])
        bank = 2
        for d in range(DC):
            b = ffn_rot.next()
            if b == bank:
                b = ffn_rot.next()
            for piece in range(4):
                s = load_panel(l * NPANEL + P_WDN + d * 4 + piece, width=11 * 128)
                for kk in range(11):
                    w, wacc = WS.ap(s, kk * 128, (kk + 1) * 128)
                    ga, gacc = G.ap(piece * 11 + kk, c0, NT)
                    mm(b, c0, NT, w, wacc, ga, gacc, start=(piece == 0 and kk == 0), stop=(piece == 3 and kk == 10))
            aa, aacc = A.ap(d, c0, NT)
            op("dve", lambda e, aa=aa, b=b: e.tensor_copy(out=aa, in_=psum[b][:, c0:NT]),
               reads=[ps_acc(b, c0, NT)], writes=[aacc])
            stats_chunk(bank, c0, psum[b][:, c0:NT], ps_acc(b, c0, NT), d, d == DC - 1)
        post_norm_apply(l, 3, c0, bank, mask_tile)

    import os as _os
    kstage = int(_os.environ.get("KSTAGE", "100000"))
    cnt = [0]

    def stage(fn, *a):
        cnt[0] += 1
        if cnt[0] <= kstage:
            fn(*a)

    for ti in range(ntiles):
        op("sp", lambda e, ti=ti: e.dma_start(out=X.t[:].rearrange("p a b -> p (a b)"), in_=xin[ti]),
           writes=[X.whole()], dma=True)
        if ti == 0:
            stage(mixer, 0, ti, 384, None)
            continue
        mask_tile = None
        if ti < HALO_TILES:
            ma, macc = TMP.ap(3, 0, NT)
            op("sp", lambda e, ma=ma, ti=ti: e.dma_start(out=ma, in_=tokmask_d[:, ti * NT:(ti + 1) * NT]),
               writes=[macc], dma=True)
            mask_tile = (ma, macc)
            tmp_rot.items = [0, 1, 2]
        if ti == 1:
            stage(mixer, 0, ti, 0, 384)
            stage(ffn, 0, ti, 384, mask_tile)
            stage(mixer, 1, ti, 448, None)
        elif ti == 2:
            stage(mixer, 0, ti, 0, 0)
            stage(ffn, 0, ti, 0, mask_tile)
            stage(mixer, 1, ti, 0, 448)
            stage(ffn_carry_prep, 1, ti, 448)
        else:
            stage(mixer, 0, ti, 0, 0)
            stage(ffn, 0, ti, 0)
            stage(mixer, 1, ti, 0, 0)
            stage(ffn, 1, ti, 0, None, ti == HALO_TILES)
            op("sp", lambda e, ti=ti: e.dma_start(out=yout[ti - HALO_TILES], in_=X.t[:].rearrange("p a b -> p (a b)")),
               reads=[X.whole()], dma=True)
        if ti < HALO_TILES:
            tmp_rot.items = [0, 1, 2, 3]

    S.finalize()
    S.emit(nc)
    return nc, S.stats


def _fm(v):
    v = np.asarray(v, dtype=np.float32)
    lead = v.shape[:-1]
    n = v.shape[-1] // 128
    v = v.reshape(lead + (n, 128))
    return np.ascontiguousarray(np.moveaxis(v, -1, 0))


def _panelize(w, kc_per_panel=None):
    K, N = w.shape
    a = w.reshape(K // 128, 128, N // 128, 128)
    a = a.transpose(2, 1, 0, 3)
    return np.ascontiguousarray(a.reshape(N // 128, 128, (K // 128) * 128))


def prepare_shared(inp):
    L = 2
    wsrc = np.zeros((L * NPANEL, 128, 2048), dtype=np.float32)
    for l in range(L):
        base = l * NPANEL
        wsrc[base + P_WIN:base + P_WIN + 32] = _panelize(np.asarray(inp["w_in"][l], dtype=np.float32))
        wsrc[base + P_WO:base + P_WO + 16] = _panelize(np.asarray(inp["w_o"][l], dtype=np.float32))
        wsrc[base + P_WUP:base + P_WUP + 88] = _panelize(np.asarray(inp["w_up"][l], dtype=np.float32))
        wd = _panelize(np.asarray(inp["w_down"][l], dtype=np.float32))
        wd = wd.reshape(16, 128, 4, 11 * 128).transpose(0, 2, 1, 3).reshape(64, 128, 11 * 128)
        wsrc[base + P_WDN:base + P_WDN + 64, :, :11 * 128] = wd
        pw = np.asarray(inp["pool_w"][l], dtype=np.float32)
        pw = pw.reshape(4, 2, 128, 2, 128).transpose(2, 0, 3, 1, 4)
        wsrc[base + P_PW] = pw.reshape(128, 2048)
    gains = np.stack([np.stack([_fm(inp[k][l]) for k in ("pre_mix_g", "post_mix_g", "pre_ffn_g", "post_ffn_g")], axis=1)
                      for l in range(L)], axis=1)
    gains = np.ascontiguousarray(gains.reshape(128, L * 4 * DC))
    cw = np.asarray(inp["conv_w"], dtype=np.float32)
    cb = np.asarray(inp["conv_b"], dtype=np.float32)
    cp = np.concatenate([cw, cb[:, None, :]], axis=1)
    cp = cp.reshape(L, 4, 88, 128).transpose(3, 0, 2, 1)
    convp = np.ascontiguousarray(cp.reshape(128, L * 88 * 4))
    ps = np.asarray(inp["pool_scale"], dtype=np.float32).reshape(L, 8, 128).transpose(2, 0, 1)
    pscale = np.ascontiguousarray(ps.reshape(128, L * 8))
    rb = np.asarray(inp["rel_bias"], dtype=np.float32)
    k = np.arange(128)[:, None]
    q = np.arange(256)[None, :]
    idx = np.clip(q - k, -128, 128) + 128
    relB = rb[:, :, idx]
    relB = np.array(relB.transpose(2, 0, 1, 3))
    relB[64:, :, :, 0:64] = NEG
    relB = np.ascontiguousarray(relB.reshape(128, L * NH * 256))
    relc = np.ascontiguousarray(np.broadcast_to(rb[:, :, 256].reshape(1, L * NH), (128, L * NH)))
    iden = np.eye(128, dtype=np.float32)
    return dict(wsrc=wsrc, gains=gains, convp=convp, pscale=pscale, relB=relB, relc=relc, iden=iden)


def prepare_core(x, core, s_core, n_main):
    ntiles = HALO_TILES + n_main
    b = core // CPB
    t0 = (core % CPB) * s_core
    lo = t0 - HALO
    ntok = ntiles * NT
    xs = np.zeros((ntok, D), dtype=np.float32)
    a = max(lo, 0)
    xs[a - lo:] = x[b, a:t0 + s_core]
    xin = xs.reshape(ntiles, NT, DC, 128).transpose(0, 3, 2, 1)
    xin = np.ascontiguousarray(xin.reshape(ntiles, 128, DC * NT))
    pos = lo + np.arange(HALO)
    valid = (pos >= 0).astype(np.float32)
    tokmask = np.ascontiguousarray(np.broadcast_to(valid[None, :], (128, HALO)))
    kmask = np.ascontiguousarray(valid.reshape(12, 128).T)
    invc = np.zeros((128, 4, 16), dtype=np.float32)
    for g, w in enumerate((2, 4, 8, 16)):
        p = t0 + np.arange(16)
        invc[:, g, :] = (1.0 / np.minimum(p + 1, w))[None, :]
    return dict(xin=xin, tokmask=tokmask, kmask=kmask, invc=np.ascontiguousarray(invc.reshape(128, 64)))


_PROG = {}


def run(inputs, s_core, trace=False):
    n_main = s_core // NT
    if n_main not in _PROG:
        _PROG[n_main] = build_program(n_main)
    nc, stats = _PROG[n_main]
    shared = prepare_shared(inputs)
    x = np.asarray(inputs["x"], dtype=np.float32)
    in_maps = []
    for c in range(NCORES):
        m = dict(shared)
        m.update(prepare_core(x, c, s_core, n_main))
        in_maps.append(m)
    res = run_bass_kernel_spmd(nc, in_maps, core_ids=list(range(NCORES)), trace=trace)
    B = x.shape[0]
    out = np.empty((B, CPB * s_core, D), dtype=np.float32)
    for c in range(NCORES):
        y = np.asarray(res.results[c]["yout"]).reshape(n_main, 128, DC, NT)
        y = y.transpose(0, 3, 2, 1).reshape(n_main * NT, D)
        b = c // CPB
        t0 = (c % CPB) * s_core
        out[b, t0:t0 + s_core] = y
    return out, res


def kernel(**inputs):
    out, _ = run(inputs, 4096)
    return out
```
